# Optimizing a Trainium2 kernel written in Bass

```python
import jax, jax.numpy as jnp
from jax import lax
import numpy as np

D_MODEL = 1024
BATCH = 8
SEQ = 8192
DEPTH = 1
DEC_BATCH = 2
DEC_SEQ = 8192
PAST_LEN = 128

GRID_W = 64
N_HEADS = 8
HEAD_DIM = 64
D_ATTN = N_HEADS * HEAD_DIM
D_CONV = 512
CONV_W = 3
WIN_ROWS_MAX = 8
WIN_COLS = 16
COL_BLOCK = 16
KEY_COL_BLOCK = COL_BLOCK + WIN_COLS
N_COL_BLOCKS = GRID_W // COL_BLOCK
N_BRANCHES = 2
D_IN_PROJ = 3 * D_ATTN + 3 * D_CONV + N_BRANCHES * D_MODEL
D_FF = ((8 * D_MODEL + 3 * 256 - 1) // (3 * 256)) * 256
RMS_EPS = 1e-6
NEG_INF = -1e30

kernel_name = "hybrid_natten_shortconv_encoder"


def rms_norm(x, g):
    xf = x.astype(jnp.float32)
    y = xf * lax.rsqrt(jnp.mean(xf * xf, axis=-1, keepdims=True) + RMS_EPS)
    return (y * g.astype(jnp.float32)).astype(x.dtype)


def neighbourhood_attention(q, k, v, rpb):
    bsz, seq_len, _ = q.shape
    rows = seq_len // GRID_W
    wr = min(WIN_ROWS_MAX, rows)
    qg = q.reshape(bsz, rows, N_COL_BLOCKS, COL_BLOCK, N_HEADS, HEAD_DIM)
    kg = k.reshape(bsz, rows, GRID_W, N_HEADS, HEAD_DIM)
    vg = v.reshape(bsz, rows, GRID_W, N_HEADS, HEAD_DIM)

    j = np.arange(N_COL_BLOCKS)
    kcol = np.clip(j * COL_BLOCK - WIN_COLS // 2, 0, GRID_W - KEY_COL_BLOCK)[:, None] + np.arange(KEY_COL_BLOCK)[None, :]
    qcol = j[:, None] * COL_BLOCK + np.arange(COL_BLOCK)[None, :]
    cstart = np.clip(qcol - WIN_COLS // 2, 0, GRID_W - WIN_COLS)
    valid = (kcol[:, None, :] >= cstart[:, :, None]) & (kcol[:, None, :] < cstart[:, :, None] + WIN_COLS)
    col_idx = np.clip(kcol[:, None, :] - qcol[:, :, None] + WIN_COLS - 1, 0, 2 * WIN_COLS - 2)
    valid = jnp.asarray(valid)
    rpb_cols = rpb[:, :, col_idx]
    scale = HEAD_DIM ** -0.5

    def row_step(r):
        rs = jnp.clip(r - wr // 2, 0, rows - wr)
        qb = lax.dynamic_index_in_dim(qg, r, axis=1, keepdims=False)
        kr = lax.dynamic_slice_in_dim(kg, rs, wr, axis=1)
        vr = lax.dynamic_slice_in_dim(vg, rs, wr, axis=1)
        kb = kr[:, :, kcol]
        vb = vr[:, :, kcol]
        s = jnp.einsum('bjqhd,bwjkhd->bhjqwk', qb, kb).astype(jnp.float32) * scale
        ri = rs + jnp.arange(wr) - r + (WIN_ROWS_MAX - 1)
        bias = jnp.take(rpb_cols, ri, axis=1).transpose(0, 2, 3, 1, 4)
        s = s + bias[None].astype(jnp.float32)
        s = jnp.where(valid[None, None, :, :, None, :], s, NEG_INF)
        sh = s.shape
        p = jax.nn.softmax(s.reshape(sh[:4] + (wr * KEY_COL_BLOCK,)), axis=-1).reshape(sh)
        return jnp.einsum('bhjqwk,bwjkhd->bjqhd', p.astype(v.dtype), vb)

    out = lax.map(row_step, jnp.arange(rows))
    return out.transpose(1, 0, 2, 3, 4, 5).reshape(bsz, seq_len, D_ATTN)


def short_conv(z, w):
    zp = jnp.pad(z, ((0, 0), (1, 1), (0, 0)))
    return zp[:, :-2] * w[0] + zp[:, 1:-1] * w[1] + zp[:, 2:] * w[2]


def mixer(xn, w_in, b_gate, rpb, conv_w, w_attn_branch, w_conv_branch, w_out):
    proj = jnp.einsum('bld,de->ble', xn, w_in)
    splits = np.cumsum([D_ATTN, D_ATTN, D_ATTN, D_CONV, D_CONV, D_CONV])
    q, k, v, u, bg, cg, gl = jnp.split(proj, splits, axis=-1)
    a = neighbourhood_attention(q, k, v, rpb)
    c = bg * short_conv(cg * u, conv_w)
    gates = jax.nn.sigmoid(gl + b_gate)
    g_a, g_c = jnp.split(gates, 2, axis=-1)
    merged = g_a * jnp.einsum('ble,ed->bld', a, w_attn_branch) + g_c * jnp.einsum('ble,ed->bld', c, w_conv_branch)
    return jnp.einsum('bld,de->ble', merged, w_out)


def swiglu(xn, w_ffn_in, w_ffn_down):
    h = jnp.einsum('bld,df->blf', xn, w_ffn_in)
    gate, up = jnp.split(h, 2, axis=-1)
    return jnp.einsum('blf,fd->bld', jax.nn.silu(gate) * up, w_ffn_down)


def trunk(x, norm_mix_g, w_in, b_gate, rpb, conv_w, w_attn_branch, w_conv_branch, w_out,
          norm_ffn_g, w_ffn_in, w_ffn_down, norm_final_g):
    for i in range(DEPTH):
        x = x + mixer(rms_norm(x, norm_mix_g[i]), w_in[i], b_gate[i], rpb[i], conv_w[i],
                      w_attn_branch[i], w_conv_branch[i], w_out[i])
        x = x + swiglu(rms_norm(x, norm_ffn_g[i]), w_ffn_in[i], w_ffn_down[i])
    return rms_norm(x, norm_final_g)


def setup_inputs(seed: int = 0) -> dict:
    key = jax.random.key(seed)
    ks = jax.random.split(key, 16)
    f32 = jnp.float32
    n = lambda k, shape, s: jax.random.normal(k, shape, f32) * s
    return {
        "x_prompt": n(ks[0], (BATCH, SEQ, D_MODEL), 1.0),
        "x_sample": n(ks[1], (DEC_BATCH, DEC_SEQ, D_MODEL), 1.0),
        "norm_mix_g": 1.0 + n(ks[2], (DEPTH, D_MODEL), 0.02),
        "w_in": n(ks[3], (DEPTH, D_MODEL, D_IN_PROJ), D_MODEL ** -0.5),
        "b_gate": n(ks[4], (DEPTH, N_BRANCHES * D_MODEL), 0.02),
        "rpb": n(ks[5], (DEPTH, N_HEADS, 2 * WIN_ROWS_MAX - 1, 2 * WIN_COLS - 1), 0.5),
        "conv_w": n(ks[6], (DEPTH, CONV_W, D_CONV), CONV_W ** -0.5),
        "w_attn_branch": n(ks[7], (DEPTH, D_ATTN, D_MODEL), D_ATTN ** -0.5),
        "w_conv_branch": n(ks[8], (DEPTH, D_CONV, D_MODEL), D_CONV ** -0.5),
        "w_out": n(ks[9], (DEPTH, D_MODEL, D_MODEL), D_MODEL ** -0.5),
        "norm_ffn_g": 1.0 + n(ks[10], (DEPTH, D_MODEL), 0.02),
        "w_ffn_in": n(ks[11], (DEPTH, D_MODEL, 2 * D_FF), D_MODEL ** -0.5),
        "w_ffn_down": n(ks[12], (DEPTH, D_FF, D_MODEL), D_FF ** -0.5),
        "norm_final_g": 1.0 + n(ks[13], (D_MODEL,), 0.02),
    }


def reference(x_prompt, x_sample, norm_mix_g, w_in, b_gate, rpb, conv_w, w_attn_branch,
              w_conv_branch, w_out, norm_ffn_g, w_ffn_in, w_ffn_down, norm_final_g):
    y_prompt = trunk(x_prompt, norm_mix_g, w_in, b_gate, rpb, conv_w, w_attn_branch, w_conv_branch,
                     w_out, norm_ffn_g, w_ffn_in, w_ffn_down, norm_final_g)
    y_sample = trunk(x_sample, norm_mix_g, w_in, b_gate, rpb, conv_w, w_attn_branch, w_conv_branch,
                     w_out, norm_ffn_g, w_ffn_in, w_ffn_down, norm_final_g)
    return (y_prompt, y_sample)
```

```python
import numpy as np
from contextlib import ExitStack
import concourse.bass as bass
import concourse.mybir as mybir
from concourse.bass_utils import run_bass_kernel_spmd

F32 = mybir.dt.float32
BF16 = mybir.dt.bfloat16
AF = mybir.ActivationFunctionType
ALU = mybir.AluOpType

D = 1024
DA = 512
DC = 512
DFF = 2816
NH = 8
GW = 64
NCORES = 8
EPS = 1e-6
NEG = -30000.0
NSLOT = 4


class Res:
    __slots__ = ("name", "w", "r")

    def __init__(self, name):
        self.name = name
        self.w = None
        self.r = {}


class Op:
    __slots__ = ("eng", "fn", "deps", "key", "val", "need", "dma")


class Sched:
    ENGS = ("sp", "pe", "act", "dve", "pool")

    def __init__(self):
        self.q = {e: [] for e in self.ENGS}
        self.dma_cnt = {}

    def op(self, eng, fn, R=(), W=(), dma=None):
        o = Op()
        o.eng, o.fn, o.dma = eng, fn, dma
        o.need = dma is not None
        o.key = dma if dma is not None else eng
        o.val = 0
        if dma is not None:
            self.dma_cnt[dma] = self.dma_cnt.get(dma, 0) + 1
            o.val = 16 * self.dma_cnt[dma]
        deps = {}

        def add(d, raw):
            if d is None or d is o:
                return
            if d.dma is None and o.dma is None and d.eng == eng:
                if eng == "pe" or not raw:
                    return
            deps[id(d)] = d

        for r in R:
            add(r.w, True)
        for r in W:
            add(r.w, False)
            for x in r.r.values():
                add(x, False)
        o.deps = list(deps.values())
        for d in o.deps:
            d.need = True
        for r in R:
            r.r[eng if dma is None else ("d", id(o))] = o
        for r in W:
            r.w = o
            r.r = {}
        self.q[eng].append(o)
        return o

    def barrier_op(self, eng, deps):
        o = Op()
        o.eng, o.fn, o.dma, o.need, o.key, o.val = eng, None, None, False, eng, 0
        o.deps = list(deps)
        for d in o.deps:
            d.need = True
        self.q[eng].append(o)
        return o

    def finalize(self):
        for e in self.ENGS:
            c = 0
            for o in self.q[e]:
                if o.dma is None and o.need and o.fn is not None:
                    c += 1
                    o.val = c

    def emit(self, eng, handle, sems):
        waited = {}
        for o in self.q[eng]:
            for d in o.deps:
                if waited.get(d.key, 0) < d.val:
                    handle.wait_ge(sems[d.key], d.val)
                    waited[d.key] = d.val
            if o.fn is None:
                continue
            ins = o.fn(handle)
            if o.need:
                ins.then_inc(sems[o.key], 16 if o.dma is not None else 1)


def build_program(NB, dbg=0):
    nc = bass.Bass("TRN2", target_bir_lowering=False)
    NBT = NB + 2
    S = Sched()

    def din(name, shape, dt=F32):
        return nc.dram_tensor(name, list(shape), dt, kind="ExternalInput").ap()

    x_d = din("x", [NBT * 512, D])
    w_in_d = din("w_in", [D, 5120])
    w_ab_d = din("w_ab", [DA, D])
    w_cb_d = din("w_cb", [DC, D])
    w_out_d = din("w_out", [D, D])
    w_fi_d = din("w_fi", [D, 2 * DFF])
    w_fd_d = din("w_fd", [DFF, D])
    g8_d = din("g8", [128, 8])
    gf8_d = din("gf8", [128, 8])
    gfin_d = din("gfin", [128, D])
    bg16_d = din("bg16", [128, 16])
    cw_d = din("cw", [128, 12])
    rpbx_d = din("rpbx", [128, NH, 15, 64])
    cv_d = din("cv", [128, 64])
    rb_d = din("rb", [128, NB * 48])
    cm_d = din("cm", [128, 2 * NBT])
    y_d = nc.dram_tensor("y", [NB * 512, D], F32, kind="ExternalOutput").ap()

    s_win = nc.dram_tensor("s_win", [10, 128, 8, 512], BF16).ap()
    s_wbr = nc.dram_tensor("s_wbr", [2, 128, 8, 512], BF16).ap()
    s_wout = nc.dram_tensor("s_wout", [2, 128, 8, 512], BF16).ap()
    s_wfi = nc.dram_tensor("s_wfi", [11, 128, 8, 2, 2, 128], BF16).ap()
    s_wfd = nc.dram_tensor("s_wfd", [3, 2, 128, 8, 512], BF16).ap()

    es = ExitStack()
    with es:
        def sb(name, shape, dt):
            return es.enter_context(nc.sbuf_tensor("sb_" + name, list(shape), dt))

        ring = [sb(f"ring{i}", [128, 4096], BF16) for i in range(NSLOT)]
        ringR = [Res(f"ring{i}") for i in range(NSLOT)]
        er = sb("er", [128, 7, 1024], BF16)
        erR = Res("er")
        rbias = sb("rbias", [128, NB * 48], F32)
        gfin = sb("gfin", [128, D], F32)
        ident = sb("ident", [128, 128], BF16)
        g8 = sb("g8", [128, 8], F32)
        gf8 = sb("gf8", [128, 8], F32)
        bg16 = sb("bg16", [128, 16], F32)
        cw = sb("cw", [128, 12], F32)
        cm = sb("cm", [128, 2 * NBT], F32)
        w0m = sb("w0m", [128, 4, NBT], F32)
        w2m = sb("w2m", [128, 4, NBT], F32)
        cvt = sb("cvt", [128, 64], F32)
        negh = sb("negh", [128, 1], F32)
        constR = Res("const")

        xnT = [sb(f"xnT{i}", [128, 8, 512], BF16) for i in range(2)]
        xnTR = [[Res(f"xnT{i}_{t}") for t in range(4)] for i in range(2)]
        kT = [sb(f"kT{i}", [128, 4, 512], BF16) for i in range(3)]
        kTR = [Res(f"kT{i}") for i in range(3)]
        vv = [sb(f"v{i}", [128, 4, NH, 65], BF16) for i in range(3)]
        vR = [[Res(f"v{i}_{t}") for t in range(4)] for i in range(3)]
        zT = [sb(f"zT{i}", [128, 4, 512], BF16) for i in range(3)]
        zTR = [Res(f"zT{i}") for i in range(3)]

        xin = [sb(f"xin{i}", [128, D], F32) for i in range(2)]
        xinR = [Res(f"xin{i}") for i in range(2)]
        xntm = [sb(f"xntm{i}", [128, D], BF16) for i in range(2)]
        xntmR = [Res(f"xntm{i}") for i in range(2)]
        xw = sb("xw", [128, 4, D], F32)
        xwR = [Res(f"xw{i}") for i in range(4)]
        qT = sb("qT", [128, 4, 2, 512], BF16)
        qTR = Res("qT")
        uT = [sb(f"uT{i}", [128, 512], F32) for i in range(2)]
        uTR = [Res(f"uT{i}") for i in range(2)]
        pexp = [sb(f"pexp{i}", [128, 1024], BF16) for i in range(2)]
        pexpR = [Res(f"pexp{i}") for i in range(2)]
        pT = [sb(f"pT{i}", [128, 1024], BF16) for i in range(2)]
        pTR = [Res(f"pT{i}") for i in range(2)]
        atm = [sb(f"atm{i}", [128, 512], BF16) for i in range(2)]
        atmR = [Res(f"atm{i}") for i in range(2)]
        aT = sb("aT", [128, 4, 512], BF16)
        aTR = Res("aT")
        cvo = [sb(f"cvo{i}", [128, 512], F32) for i in range(2)]
        cvoR = [Res(f"cvo{i}") for i in range(2)]
        cT = sb("cT", [128, 4, 512], BF16)
        cTR = Res("cT")
        ga = [sb(f"ga{i}", [128, 512], F32) for i in range(2)]
        gaR = [Res(f"ga{i}") for i in range(2)]
        gc = [sb(f"gc{i}", [128, 512], F32) for i in range(2)]
        gcR = [Res(f"gc{i}") for i in range(2)]
        mx = sb("mx", [128, 8, 512], BF16)
        mxR = [Res(f"mx{i}") for i in range(4)]
        sg = [sb(f"sg{i}", [128, 512], F32) for i in range(2)]
        sgR = [Res(f"sg{i}") for i in range(2)]
        hT = sb("hT", [128, 8, 512], BF16)
        hTR = Res("hT")
        NST = 8
        st = sb("st", [128, NST, 4], F32)
        stR = [Res(f"st{i}") for i in range(NST)]
        rden = [sb(f"rden{i}", [128, 8], F32) for i in range(2)]
        rdenR = [Res(f"rden{i}") for i in range(2)]
        egh = [sb(f"egh{i}", [128, 15, 64], F32) for i in range(2)]
        eghR = [Res(f"egh{i}") for i in range(2)]

        ps = es.enter_context(nc.psum_tensor("ps", [128, 4096], F32))
        bankR = [Res(f"bank{i}") for i in range(8)]

        def bank(k):
            return ps[:, k * 512:(k + 1) * 512]

        def bank_bf(k):
            return ps[:, k * 512:(k + 1) * 512].bitcast(BF16)

        cnt = {"mm": 0, "tp": 0, "st": 0}

        def next_mm():
            k = cnt["mm"] % 6
            cnt["mm"] += 1
            return k

        def next_tp():
            k = cnt["tp"] % 2
            cnt["tp"] += 1
            return k

        def next_st():
            k = cnt["st"] % NST
            cnt["st"] += 1
            return k

        def cload(dst, src, key):
            S.op("sp", lambda e, d=dst, s=src: e.dma_start(out=d, in_=s), W=[constR], dma=key)

        cload(rbias[:, :], rb_d, "c0")
        cload(gfin[:, :], gfin_d, "c1")
        cload(g8[:, :], g8_d, "c2")
        cload(gf8[:, :], gf8_d, "c3")
        cload(bg16[:, :], bg16_d, "c4")
        cload(cw[:, :], cw_d, "c5")
        cload(cm[:, :], cm_d, "c6")
        cload(cvt[:, :], cv_d, "c7")
        ident_d = din("ident", [128, 128], BF16)
        cload(ident[:, :], ident_d, "c8")
        S.op("pool", lambda e: e.memset(qT[:, :, :, :], 0.0), W=[qTR])
        S.op("pool", lambda e: e.memset(negh[:, :], -0.5), W=[constR])
        for i in range(3):
            S.op("pool", lambda e, i=i: e.memset(vv[i][:, :, :, 64:65], 1.0), W=vR[i])
        cm3 = cm[:, :].rearrange("p (b t) -> p b t", t=2)
        for c in range(4):
            S.op("dve", lambda e, c=c: e.tensor_scalar(out=w0m[:, c, :], in0=cm3[:, :, 0], scalar1=cw[:, c * 3:c * 3 + 1],
                                                        scalar2=None, op0=ALU.mult), R=[constR], W=[constR])
            S.op("dve", lambda e, c=c: e.tensor_scalar(out=w2m[:, c, :], in0=cm3[:, :, 1], scalar1=cw[:, c * 3 + 2:c * 3 + 3],
                                                        scalar2=None, op0=ALU.mult), R=[constR], W=[constR])
        er5 = er[:, :, :].rearrange("p o (h r q) -> p o h r q", h=NH, r=2)
        for h in range(NH):
            sl = h % 2
            S.op("sp", lambda e, h=h, sl=sl: e.dma_start(out=egh[sl][:, :, :], in_=rpbx_d[:, h, :, :]),
                 W=[eghR[sl]], dma=f"egh{sl}")
            S.op("act", lambda e, sl=sl: e.activation(out=egh[sl][:, :, :], in_=egh[sl][:, :, :], func=AF.Exp),
                 R=[eghR[sl]], W=[eghR[sl]])
            for o in range(-3, 4):
                for qr in range(2):
                    for kr in range(2):
                        dr = 2 * o + kr - qr + 7
                        p0, p1 = kr * 64, kr * 64 + 64
                        S.op("dve", lambda e, o=o, qr=qr, p0=p0, p1=p1, dr=dr, h=h, sl=sl: e.tensor_tensor(
                            out=er5[p0:p1, o + 3, h, qr, :], in0=egh[sl][p0:p1, dr, :], in1=cvt[p0:p1, :], op=ALU.mult),
                            R=[eghR[sl], constR], W=[erR])

        xw_flat = xw[:, :, :].rearrange("p a b -> p (a b)")
        def f32view(t_):
            return t_[:, :, :].rearrange("p a b -> p (a b)").bitcast(F32)
        prep_in = [xw_flat[:, 0:2048], xw_flat[:, 2048:4096], f32view(xnT[0]), f32view(xnT[1]), f32view(hT), f32view(mx)]
        NPI = len(prep_in)
        prep_inR = [Res(f"pin{i}") for i in range(NPI)]
        NPO = 2 * NSLOT
        prep_out = [ring[i // 2][:, (i % 2) * 2048:(i % 2) * 2048 + 2048] for i in range(NPO)]
        prep_outR = [Res(f"pout{i}") for i in range(NPO)]
        units = []
        for kc in range(8):
            for (c0, n) in ((0, 2048), (2048, 1024)):
                g0, ng = c0 // 512, n // 512
                dst = s_win[g0:g0 + ng, :, kc, :].rearrange("g p w -> p g w")
                units.append((w_in_d[kc * 128:(kc + 1) * 128, c0:c0 + n], n, dst, ("g", ng, 512), g8[:, kc:kc + 1]))
            for u in range(2):
                c0 = 3072 + u * 1024
                dst = s_win[6:10, :, kc, :].rearrange("g p (j u w) -> p g j u w", j=2, u=2)[:, :, :, u, :]
                units.append((w_in_d[kc * 128:(kc + 1) * 128, c0:c0 + 1024], 1024, dst, ("gj", 4, 2, 128), g8[:, kc:kc + 1]))
        for kc in range(4):
            units.append((w_ab_d[kc * 128:(kc + 1) * 128, :], 1024, s_wbr[:, :, kc, :].rearrange("g p w -> p g w"),
                          ("g", 2, 512), None))
            units.append((w_cb_d[kc * 128:(kc + 1) * 128, :], 1024, s_wbr[:, :, 4 + kc, :].rearrange("g p w -> p g w"),
                          ("g", 2, 512), None))
        for kc in range(8):
            dst = s_wout[:, :, kc, :].rearrange("g p w -> p g w")
            units.append((w_out_d[kc * 128:(kc + 1) * 128, :], 1024, dst, ("g", 2, 512), None))
        for kc in range(8):
            for gu in range(2):
                for (j0, nj) in ((0, 16), (16, 6)):
                    c0 = gu * DFF + j0 * 128
                    gi0, ngi = j0 // 2, nj // 2
                    dst = s_wfi[gi0:gi0 + ngi, :, kc, :, gu, :].rearrange("g p j w -> p g j w")
                    units.append((w_fi_d[kc * 128:(kc + 1) * 128, c0:c0 + nj * 128], nj * 128, dst,
                                  ("gj", ngi, 2, 128), gf8[:, kc:kc + 1]))
        for kc in range(22):
            ph, kk = kc // 8, kc % 8
            dst = s_wfd[ph, :, :, kk, :].rearrange("n p w -> p n w")
            units.append((w_fd_d[kc * 128:(kc + 1) * 128, :], 1024, dst, ("g", 2, 512), None))

        prep_last = {}
        ceng = ("dve", "act")
        if dbg == 1:
            units = []
        for ui, (src, n, dst, shp, sc) in enumerate(units):
            si = ui % NPI
            ro = ui % NPO
            S.op("sp", lambda e, si=si, n=n, src=src: e.dma_start(out=prep_in[si][:, 0:n], in_=src),
                 W=[prep_inR[si]], dma=f"pin{si}")
            ce = ceng[ui % 2]
            o_ap = prep_out[ro][:, 0:n]
            i_ap = prep_in[si][:, 0:n]
            if ce == "act":
                if sc is None:
                    fn = lambda e, o_ap=o_ap, i_ap=i_ap: e.activation(out=o_ap, in_=i_ap, func=AF.Copy)
                else:
                    fn = lambda e, o_ap=o_ap, i_ap=i_ap, sc=sc: e.activation(out=o_ap, in_=i_ap, func=AF.Copy, scale=sc)
            else:
                if sc is None:
                    fn = lambda e, o_ap=o_ap, i_ap=i_ap: e.tensor_copy(out=o_ap, in_=i_ap)
                else:
                    fn = lambda e, o_ap=o_ap, i_ap=i_ap, sc=sc: e.tensor_scalar(out=o_ap, in0=i_ap, scalar1=sc,
                                                                                scalar2=None, op0=ALU.mult)
            S.op(ce, fn, R=[prep_inR[si], constR], W=[prep_outR[ro]])
            if shp is None:
                s_ap = o_ap
            elif shp[0] == "g":
                s_ap = o_ap.rearrange("p (g w) -> p g w", g=shp[1])
            else:
                s_ap = o_ap.rearrange("p (g j w) -> p g j w", g=shp[1], j=shp[2])
                prep_last[("u", ro)] = S.op("pool", lambda e, dst=dst, s_ap=s_ap: e.dma_start(out=dst[:, :, 0, :],
                                                                                          in_=s_ap[:, :, 0, :]),
                                            R=[prep_outR[ro]], dma=f"psu{ro}")
                dst = dst[:, :, 1, :]
                s_ap = s_ap[:, :, 1, :]
            prep_last[ro] = S.op("pool", lambda e, dst=dst, s_ap=s_ap: e.dma_start(out=dst, in_=s_ap),
                                 R=[prep_outR[ro]], dma=f"pst{ro}")
        last_stores = list(prep_last.values())
        for e_ in S.ENGS:
            if last_stores:
                S.barrier_op(e_, last_stores)

        wseq = []

        def win_ap(g):
            return s_win[g].rearrange("p k w -> p (k w)")

        def step_groups(do_a, do_b):
            out = []
            if do_a:
                out += [("win", 1), ("win", 2), ("win", 3), ("win", 5)]
            if do_b:
                out += [("win", 0), ("win", 4), ("wbr", 0), ("win", 6), ("win", 7), ("wbr", 1), ("win", 8), ("win", 9),
                        ("wout", 0), ("wout", 1)]
                for ph in range(3):
                    for gi in range(4 if ph < 2 else 3):
                        out.append(("wfi", ph * 4 + gi))
                    out += [("wfd", ph, 0), ("wfd", ph, 1)]
            return out

        def group_src(gk):
            if gk[0] == "win":
                return win_ap(gk[1])
            if gk[0] == "wbr":
                return s_wbr[gk[1]].rearrange("p k w -> p (k w)")
            if gk[0] == "wout":
                return s_wout[gk[1]].rearrange("p k w -> p (k w)")
            if gk[0] == "wfi":
                return s_wfi[gk[1]].rearrange("p k j u w -> p (k j u w)")
            if gk[0] == "wfd":
                return s_wfd[gk[1], gk[2]].rearrange("p k w -> p (k w)")
            raise ValueError(gk)

        for t in range(NBT):
            wseq += step_groups(True, 1 <= t - 1 <= NB)
        wstate = {"use": 0, "load": 0}

        def wnext(expect, keep=0):
            u = wstate["use"]
            if dbg != 0:
                sl = u % NSLOT
                src = group_src(expect)
                S.op("sp", lambda e, sl=sl, src=src: e.dma_start(out=ring[sl][:, :], in_=src),
                     W=[ringR[sl]], dma=f"ring{sl}")
                wstate["use"] += 1
                return ring[sl], ringR[sl]
            assert wseq[u] == expect, (wseq[u], expect)
            while wstate["load"] < len(wseq) and wstate["load"] < u - keep + NSLOT:
                li = wstate["load"]
                sl = li % NSLOT
                src = group_src(wseq[li])
                S.op("sp", lambda e, sl=sl, src=src: e.dma_start(out=ring[sl][:, :], in_=src),
                     W=[ringR[sl]], dma=f"ring{sl}")
                wstate["load"] += 1
            wstate["use"] += 1
            return ring[u % NSLOT], ringR[u % NSLOT]

        def mm_group(out_ap, bankres, pairs, extraR):
            n = len(pairs)
            last = None
            for k, (l, r) in enumerate(pairs):
                last = S.op("pe", lambda e, l=l, r=r, k=k: e.matmul(out_ap, l, r, start=(k == 0), stop=(k == n - 1)),
                            R=extraR, W=[bankres])
            return last

        def rstd_of(src_ap, srcR, junk_ap, junkR):
            k = next_st()
            S.op("act", lambda e: e.activation(out=junk_ap, in_=src_ap, func=AF.Square, accum_out=st[:, k, 0:1]),
                 R=[srcR], W=[junkR, stR[k]])
            S.op("pool", lambda e: e.tensor_scalar(out=st[:, k, 1:2], in0=st[:, k, 0:1], scalar1=1.0 / D, scalar2=EPS,
                                                   op0=ALU.mult, op1=ALU.add), R=[stR[k]], W=[stR[k]])
            S.op("pool", lambda e: e.tensor_tensor(out=st[:, k, 2:3], in0=st[:, k, 1:2], in1=negh[:, 0:1], op=ALU.pow),
                 R=[stR[k], constR], W=[stR[k]])
            return k

        def transpose_to(src_tm, srcR, nchunk, dst_ap_fn, dstR, evac_eng, tb=None):
            if tb is None:
                tb = next_tp()
            pb = bank_bf(tb)
            for c in range(nchunk):
                S.op("pe", lambda e, c=c: e.transpose(pb[:, c * 128:(c + 1) * 128], src_tm[:, c * 128:(c + 1) * 128],
                                                      ident[:, :]), R=[srcR, constR], W=[bankR[tb]])
            src3 = pb[:, 0:nchunk * 128].rearrange("p (c t) -> p c t", c=nchunk)
            if evac_eng == "act":
                S.op("act", lambda e: e.activation(out=dst_ap_fn(), in_=src3, func=AF.Copy), R=[bankR[tb]], W=[dstR])
            else:
                S.op("dve", lambda e: e.tensor_copy(out=dst_ap_fn(), in_=src3), R=[bankR[tb]], W=[dstR])

        XQ = "pool"

        def preA_ew(b, i):
            xi = (b * 4 + i) % 2
            tok0 = (b * 4 + i) * 128
            S.op(XQ, lambda e, xi=xi, tok0=tok0: e.dma_start(out=xin[xi][:, :], in_=x_d[tok0:tok0 + 128, :]),
                 W=[xinR[xi]], dma=f"xin{xi}")
            k = rstd_of(xin[xi][:, :], xinR[xi], xntm[xi][:, :], xntmR[xi])
            S.op("dve", lambda e, xi=xi, k=k: e.tensor_scalar(out=xntm[xi][:, :], in0=xin[xi][:, :],
                                                             scalar1=st[:, k, 2:3], scalar2=None, op0=ALU.mult),
                 R=[xinR[xi], stR[k]], W=[xntmR[xi]])

        def preA_tr(b, i):
            xs = b % 2
            xi = (b * 4 + i) % 2
            transpose_to(xntm[xi], xntmR[xi], 8, lambda i=i: xnT[xs][:, :, i * 128:(i + 1) * 128], xnTR[xs][i], "dve")

        def matA(b):
            xs = b % 2
            ks = b % 3
            xr = xnTR[xs]
            wt, wr = wnext(("win", 1))
            w3 = wt[:, :].rearrange("p (k w) -> p k w", k=8)
            for hc in range(4):
                bk = next_mm()
                mm_group(bank(bk), bankR[bk], [(w3[:, kc, hc * 128:(hc + 1) * 128], xnT[xs][:, kc, :]) for kc in range(8)],
                         [wr] + xr)
                S.op("act", lambda e, hc=hc, bk=bk: e.activation(out=kT[ks][:, hc, :], in_=bank(bk), func=AF.Copy),
                     R=[bankR[bk]], W=[kTR[ks]])
            wt, wr = wnext(("win", 2))
            w3 = wt[:, :].rearrange("p (k w) -> p k w", k=8)
            for i in range(4):
                bk = next_mm()
                mm_group(bank(bk), bankR[bk], [(xnT[xs][:, kc, i * 128:(i + 1) * 128], w3[:, kc, :]) for kc in range(8)],
                         [wr, xr[i]])
                S.op("dve", lambda e, i=i, bk=bk: e.tensor_copy(out=vv[ks][:, i, :, 0:64],
                                                               in_=bank(bk).rearrange("p (h d) -> p h d", h=NH)),
                     R=[bankR[bk]], W=[vR[ks][i]])
            wt, wr = wnext(("win", 3))
            wu = wt[:, :].rearrange("p (k w) -> p k w", k=8)
            wt2, wr2 = wnext(("win", 5), keep=1)
            wc = wt2[:, :].rearrange("p (k w) -> p k w", k=8)
            for c in range(4):
                us = c % 2
                bk = next_mm()
                mm_group(bank(bk), bankR[bk], [(wu[:, kc, c * 128:(c + 1) * 128], xnT[xs][:, kc, :]) for kc in range(8)],
                         [wr] + xr)
                S.op("act", lambda e, us=us, bk=bk: e.activation(out=uT[us][:, :], in_=bank(bk), func=AF.Copy),
                     R=[bankR[bk]], W=[uTR[us]])
                bk2 = next_mm()
                mm_group(bank(bk2), bankR[bk2], [(wc[:, kc, c * 128:(c + 1) * 128], xnT[xs][:, kc, :]) for kc in range(8)],
                         [wr2] + xr)
                S.op("dve", lambda e, us=us, bk2=bk2, c=c: e.tensor_tensor(out=zT[ks][:, c, :], in0=bank(bk2), in1=uT[us][:, :],
                                                                        op=ALU.mult),
                     R=[bankR[bk2], uTR[us]], W=[zTR[ks]])

        def B_front(b):
            xs = b % 2
            ks = b % 3
            xr = xnTR[xs]
            for i in range(4):
                tok0 = (b * 4 + i) * 128
                S.op(XQ, lambda e, i=i, tok0=tok0: e.dma_start(out=xw[:, i, :], in_=x_d[tok0:tok0 + 128, :]),
                     W=[xwR[i]], dma=f"xw{i}")
            wt, wr = wnext(("win", 0))
            w3 = wt[:, :].rearrange("p (k w) -> p k w", k=8)
            for hc in range(4):
                bk = next_mm()
                mm_group(bank(bk), bankR[bk], [(w3[:, kc, hc * 128:(hc + 1) * 128], xnT[xs][:, kc, :]) for kc in range(8)],
                         [wr] + xr)
                for hh in range(2):
                    S.op("act", lambda e, hc=hc, bk=bk, hh=hh: e.activation(
                        out=qT[hh * 64:hh * 64 + 64, hc, hh, :], in_=bank(bk)[hh * 64:hh * 64 + 64, :], func=AF.Copy, scale=0.125),
                        R=[bankR[bk]], W=[qTR])
        def conv_ew(b, c):
            ks = b % 3
            zp, zn = (b - 1) % 3, (b + 1) % 3
            cs = c % 2
            S.op("dve", lambda e: e.tensor_scalar(out=cvo[cs][:, :], in0=zT[ks][:, c, :],
                                                  scalar1=cw[:, c * 3 + 1:c * 3 + 2], scalar2=None, op0=ALU.mult),
                 R=[zTR[ks], constR], W=[cvoR[cs]])
            S.op("dve", lambda e: e.scalar_tensor_tensor(out=cvo[cs][:, 1:512], in0=zT[ks][:, c, 0:511],
                                                         scalar=cw[:, c * 3:c * 3 + 1], in1=cvo[cs][:, 1:512],
                                                         op0=ALU.mult, op1=ALU.add),
                 R=[zTR[ks], constR, cvoR[cs]], W=[cvoR[cs]])
            S.op("dve", lambda e: e.scalar_tensor_tensor(out=cvo[cs][:, 0:1], in0=zT[zp][:, c, 511:512],
                                                         scalar=w0m[:, c, b:b + 1], in1=cvo[cs][:, 0:1],
                                                         op0=ALU.mult, op1=ALU.add),
                 R=[zTR[zp], constR, cvoR[cs]], W=[cvoR[cs]])
            S.op("dve", lambda e: e.scalar_tensor_tensor(out=cvo[cs][:, 511:512], in0=zT[zn][:, c, 0:1],
                                                         scalar=w2m[:, c, b:b + 1], in1=cvo[cs][:, 511:512],
                                                         op0=ALU.mult, op1=ALU.add),
                 R=[zTR[zn], constR, cvoR[cs]], W=[cvoR[cs]])
            S.op("dve", lambda e: e.scalar_tensor_tensor(out=cT[:, c, 0:511], in0=zT[ks][:, c, 1:512],
                                                         scalar=cw[:, c * 3 + 2:c * 3 + 3], in1=cvo[cs][:, 0:511],
                                                         op0=ALU.mult, op1=ALU.add),
                 R=[zTR[ks], constR, cvoR[cs]], W=[cTR])
            S.op("dve", lambda e: e.tensor_copy(out=cT[:, c, 511:512], in_=cvo[cs][:, 511:512]),
                 R=[cvoR[cs]], W=[cTR])

        def B_bg(b):
            xs = b % 2
            xr = xnTR[xs]
            wt, wr = wnext(("win", 4))
            w3 = wt[:, :].rearrange("p (k w) -> p k w", k=8)
            for c in range(4):
                bk = c % 2
                mm_group(bank(bk), bankR[bk], [(w3[:, kc, c * 128:(c + 1) * 128], xnT[xs][:, kc, :]) for kc in range(8)],
                         [wr] + xr)
                S.op("dve", lambda e, c=c, bk=bk: e.tensor_tensor(out=cT[:, c, :], in0=bank(bk), in1=cT[:, c, :], op=ALU.mult),
                     R=[bankR[bk], cTR], W=[cTR])

        def B_attn(b):
            chunks = []
            for i in range(4):
                olist = [-2, -1, 0, 1, 2]
                if i == 0:
                    olist = olist + [3]
                if i == 3:
                    olist = [-3] + olist
                olist = sorted(olist)
                for s_, o in enumerate(olist):
                    chunks.append((i, s_, o, len(olist)))

            def sbanks_of(n):
                return (2, 3) if n % 2 == 0 else (4, 5)

            def emit_S(n):
                i, s_, o, L = chunks[n]
                g2 = 4 * b + i + o
                b2, i2 = g2 // 4, g2 % 4
                k2 = b2 % 3
                sbk = sbanks_of(n)
                for h in range(NH):
                    for qr in range(2):
                        oap = bank(sbk[qr])[:, h * 64:(h + 1) * 64]
                        S.op("pe", lambda e, oap=oap, h=h, k2=k2, i2=i2, i=i, qr=qr: e.matmul(
                            oap, kT[k2][:, h // 2, i2 * 128:(i2 + 1) * 128],
                            qT[:, h // 2, h % 2, i * 128 + qr * 64:i * 128 + qr * 64 + 64], start=True, stop=True),
                            R=[kTR[k2], qTR], W=[bankR[sbk[qr]]])

            def emit_rest(n):
                i, s_, o, L = chunks[n]
                g2 = 4 * b + i + o
                b2, i2 = g2 // 4, g2 % 4
                k2 = b2 % 3
                sb_ = n % 2
                sbk = sbanks_of(n)
                tl = (b - 1) * 4 + i
                pvb = (6, 7) if i % 2 == 0 else (0, 1)
                pe4 = pexp[sb_][:, :].rearrange("p (h r q) -> p h r q", h=NH, r=2)
                for qr in range(2):
                    col = tl * 12 + s_ * 2 + qr
                    S.op("act", lambda e, pe4=pe4, qr=qr, col=col, bk=sbk[qr]: e.activation(
                        out=pe4[:, :, qr, :], in_=bank(bk).rearrange("p (h q) -> p h q", h=NH), func=AF.Exp,
                        bias=rbias[:, col:col + 1]),
                        R=[bankR[sbk[qr]], constR], W=[pexpR[sb_]])
                S.op("dve", lambda e, sb_=sb_, o=o: e.tensor_tensor(out=pT[sb_][:, :], in0=pexp[sb_][:, :], in1=er[:, o + 3, :],
                                                                   op=ALU.mult),
                     R=[pexpR[sb_], erR], W=[pTR[sb_]])
                for h in range(NH):
                    bk = pvb[h // 4]
                    oap = bank(bk)[:, (h % 4) * 65:(h % 4) * 65 + 65]
                    S.op("pe", lambda e, oap=oap, sb_=sb_, h=h, k2=k2, i2=i2, st_=(s_ == 0 and h % 4 == 0),
                         sp_=(s_ == L - 1): e.matmul(
                        oap, pT[sb_][:, h * 128:(h + 1) * 128], vv[k2][:, i2, h, :], start=st_, stop=sp_,
                        skip_group_check=True),
                        R=[pTR[sb_], vR[k2][i2]], W=[bankR[bk]])
                if s_ == L - 1:
                    ri = i % 2
                    pv4 = ps[:, pvb[0] * 512:pvb[0] * 512 + 1024].rearrange("p (b w) -> p b w", b=2)[:, :, 0:260].rearrange(
                        "p b (h d) -> p b h d", h=4)
                    S.op("dve", lambda e, ri=ri, pv4=pv4: e.reciprocal(out=rden[ri][:, :].rearrange("p (b h) -> p b h", b=2),
                                                                      in_=pv4[:, :, :, 64]),
                         R=[bankR[pvb[0]], bankR[pvb[1]]], W=[rdenR[ri]])
                    for hb in range(2):
                        S.op("dve", lambda e, ri=ri, hb=hb, pv4=pv4: e.tensor_tensor(
                            out=atm[ri][:, hb * 256:(hb + 1) * 256].rearrange("p (h d) -> p h d", h=4),
                            in0=pv4[:, hb, :, 0:64],
                            in1=rden[ri][:, hb * 4:(hb + 1) * 4].unsqueeze(2).to_broadcast([128, 4, 64]), op=ALU.mult),
                            R=[bankR[pvb[hb]], rdenR[ri]], W=[atmR[ri]])
                    deferred.append(lambda i=i, ri=ri, pvb=pvb: transpose_to(
                        atm[ri], atmR[ri], 4, lambda: aT[:, :, i * 128:(i + 1) * 128], aTR, "dve", tb=pvb[0]))

            deferred = []
            emit_S(0)
            for n in range(len(chunks)):
                if n + 1 < len(chunks):
                    emit_S(n + 1)
                if n == 0:
                    B_bg(b)
                pending = list(deferred)
                del deferred[:]
                emit_rest(n)
                for f_ in pending:
                    f_()
            for f_ in deferred:
                f_()


        def B_merge(b):
            xs = b % 2
            xr = xnTR[xs]
            cnt["mm"] = 2
            mcount = [0]

            def next_mm_m():
                mcount[0] += 1
                if mcount[0] <= 8:
                    return 2 + (mcount[0] - 1) % 4
                if mcount[0] == 9:
                    cnt["mm"] = 0
                return next_mm()
            for half in range(2):
                wbr_t, wbr_r = wnext(("wbr", half))
                wbr3 = wbr_t[:, :].rearrange("p (k w) -> p k w", k=8)
                for k2 in range(2):
                    kq = 2 * half + k2
                    wg_t, wg_r = wnext(("win", 6 + kq), keep=1 + k2)
                    wg = wg_t[:, :].rearrange("p (k j u w) -> p k j u w", k=8, j=2, u=2)
                    for j in range(2):
                        c = 2 * kq + j
                        cc = c % 4
                        gs = c % 2
                        bk = next_mm_m()
                        mm_group(bank(bk), bankR[bk], [(wg[:, kc, j, 0, :], xnT[xs][:, kc, :]) for kc in range(8)], [wg_r] + xr)
                        S.op("act", lambda e, gs=gs, bk=bk, c=c: e.activation(out=ga[gs][:, :], in_=bank(bk), func=AF.Sigmoid,
                                                                            bias=bg16[:, c:c + 1]),
                             R=[bankR[bk], constR], W=[gaR[gs]])
                        bk = next_mm_m()
                        mm_group(bank(bk), bankR[bk], [(wg[:, kc, j, 1, :], xnT[xs][:, kc, :]) for kc in range(8)], [wg_r] + xr)
                        S.op("act", lambda e, gs=gs, bk=bk, c=c: e.activation(out=gc[gs][:, :], in_=bank(bk), func=AF.Sigmoid,
                                                                            bias=bg16[:, 8 + c:9 + c]),
                             R=[bankR[bk], constR], W=[gcR[gs]])
                        bk = next_mm_m()
                        mm_group(bank(bk), bankR[bk], [(wbr3[:, kc, cc * 128:(cc + 1) * 128], aT[:, kc, :]) for kc in range(4)],
                                 [wbr_r, aTR])
                        S.op("dve", lambda e, gs=gs, bk=bk: e.tensor_tensor(out=ga[gs][:, :], in0=bank(bk), in1=ga[gs][:, :],
                                                                           op=ALU.mult),
                             R=[bankR[bk], gaR[gs]], W=[gaR[gs]])
                        bk = next_mm_m()
                        mm_group(bank(bk), bankR[bk], [(wbr3[:, 4 + kc, cc * 128:(cc + 1) * 128], cT[:, kc, :]) for kc in range(4)],
                                 [wbr_r, cTR])
                        S.op("dve", lambda e, gs=gs, bk=bk: e.tensor_tensor(out=gc[gs][:, :], in0=bank(bk), in1=gc[gs][:, :],
                                                                           op=ALU.mult),
                             R=[bankR[bk], gcR[gs]], W=[gcR[gs]])
                        S.op("pool", lambda e, gs=gs, c=c: e.tensor_tensor(out=mx[:, c, :], in0=ga[gs][:, :], in1=gc[gs][:, :],
                                                                          op=ALU.add),
                             R=[gaR[gs], gcR[gs]], W=mxR)

        def B_wout(b):
            w0t, w0r = wnext(("wout", 0))
            w1t, w1r = wnext(("wout", 1), keep=1)
            wo = [w0t[:, :].rearrange("p (k w) -> p k w", k=8), w1t[:, :].rearrange("p (k w) -> p k w", k=8)]
            wor = [w0r, w1r]
            pend = None
            for i in range(4):
                for n in range(2):
                    bk = next_mm()
                    mm_group(bank(bk), bankR[bk], [(mx[:, kc, i * 128:(i + 1) * 128], wo[n][:, kc, :]) for kc in range(8)],
                             [wor[n], mxR[i]])
                    S.op("dve", lambda e, i=i, n=n, bk=bk: e.tensor_tensor(out=xw[:, i, n * 512:(n + 1) * 512], in0=bank(bk),
                                                                         in1=xw[:, i, n * 512:(n + 1) * 512], op=ALU.add),
                         R=[bankR[bk], xwR[i]], W=[xwR[i]])
                if pend is not None:
                    pend()
                xi = i % 2
                k = rstd_of(xw[:, i, :], xwR[i], xntm[xi][:, :], xntmR[xi])
                S.op("dve", lambda e, xi=xi, k=k, i=i: e.tensor_scalar(out=xntm[xi][:, :], in0=xw[:, i, :],
                                                                      scalar1=st[:, k, 2:3], scalar2=None, op0=ALU.mult),
                     R=[xwR[i], stR[k]], W=[xntmR[xi]])
                pend = (lambda i=i, xi=xi: transpose_to(xntm[xi], xntmR[xi], 8, lambda: mx[:, :, i * 128:(i + 1) * 128],
                                                        mxR[i], "dve"))
            return pend

        def B_ffn(b, hooks):
            stores = []
            for ph in range(3):
                ngi = 4 if ph < 2 else 3
                for gi in range(ngi):
                    wt, wr = wnext(("wfi", ph * 4 + gi))
                    w5 = wt[:, :].rearrange("p (k j u w) -> p k j u w", k=8, j=2, u=2)
                    G = ph * 4 + gi
                    if G in hooks:
                        for f_ in hooks[G][0]:
                            f_()
                    for jj in range(2):
                        jl = gi * 2 + jj
                        ss = jl % 2
                        bk = next_mm()
                        mm_group(bank(bk), bankR[bk], [(w5[:, kc, jj, 0, :], mx[:, kc, :]) for kc in range(8)], [wr] + mxR)
                        S.op("act", lambda e, ss=ss, bk=bk: e.activation(out=sg[ss][:, :], in_=bank(bk), func=AF.Silu),
                             R=[bankR[bk]], W=[sgR[ss]])
                        bk = next_mm()
                        mm_group(bank(bk), bankR[bk], [(w5[:, kc, jj, 1, :], mx[:, kc, :]) for kc in range(8)], [wr] + mxR)
                        S.op("dve", lambda e, ss=ss, bk=bk, jl=jl: e.tensor_tensor(out=hT[:, jl, :], in0=bank(bk), in1=sg[ss][:, :],
                                                                                 op=ALU.mult),
                             R=[bankR[bk], sgR[ss]], W=[hTR])
                    if G in hooks:
                        for f_ in hooks[G][1]:
                            f_()
                nk = 2 * ngi
                w0t, w0r = wnext(("wfd", ph, 0))
                w1t, w1r = wnext(("wfd", ph, 1), keep=1)
                wd = [w0t[:, :].rearrange("p (k w) -> p k w", k=8), w1t[:, :].rearrange("p (k w) -> p k w", k=8)]
                wdr = [w0r, w1r]
                for i in range(4):
                    for n in range(2):
                        bk = next_mm()
                        mm_group(bank(bk), bankR[bk], [(hT[:, kc, i * 128:(i + 1) * 128], wd[n][:, kc, :]) for kc in range(nk)],
                                 [wdr[n], hTR])
                        S.op("dve", lambda e, i=i, n=n, bk=bk: e.tensor_tensor(out=xw[:, i, n * 512:(n + 1) * 512], in0=bank(bk),
                                                                             in1=xw[:, i, n * 512:(n + 1) * 512], op=ALU.add),
                             R=[bankR[bk], xwR[i]], W=[xwR[i]])
                    if ph == 2:
                        xi = i % 2
                        k = rstd_of(xw[:, i, :], xwR[i], xntm[xi][:, :], xntmR[xi])
                        S.op("dve", lambda e, k=k, i=i: e.scalar_tensor_tensor(out=xw[:, i, :], in0=xw[:, i, :], scalar=st[:, k, 2:3],
                                                                              in1=gfin[:, :], op0=ALU.mult, op1=ALU.mult),
                             R=[xwR[i], stR[k], constR], W=[xwR[i]])
                        tok0 = ((b - 1) * 4 + i) * 128
                        stores.append(S.op(XQ, lambda e, i=i, tok0=tok0: e.dma_start(out=y_d[tok0:tok0 + 128, :], in_=xw[:, i, :]),
                                           R=[xwR[i]], dma=f"xw{i}"))
            return stores

        final_stores = []
        for i in range(4):
            preA_ew(0, i)
            preA_tr(0, i)
        for t in range(NBT):
            matA(t)
            hasB = 1 <= t - 1 <= NB
            nxt = t + 1 < NBT
            if hasB:
                for c in range(4):
                    conv_ew(t - 1, c)
                B_front(t - 1)
                B_attn(t - 1)
                B_merge(t - 1)
                pend = B_wout(t - 1)
                pend()
                hooks = {}
                if nxt:
                    ew = [(lambda i=i, t=t: preA_ew(t + 1, i)) for i in range(4)]
                    tr = [(lambda i=i, t=t: preA_tr(t + 1, i)) for i in range(4)]
                    hooks = {0: ([ew[0]], []), 1: ([ew[1]], [tr[0]]), 2: ([ew[2]], [tr[1]]), 3: ([ew[3]], [tr[2]]),
                             4: ([], [tr[3]])}
                final_stores = B_ffn(t - 1, hooks)
            elif nxt:
                for i in range(4):
                    preA_ew(t + 1, i)
                    preA_tr(t + 1, i)
        assert wstate["use"] == len(wseq)
        S.barrier_op("sp", final_stores)
        S.finalize()

        keys = set()
        for e_ in S.ENGS:
            for o in S.q[e_]:
                keys.add(o.key)
        sems = {k: es.enter_context(nc.semaphore(f"s_{k}")) for k in sorted(keys)}
        block = es.enter_context(nc.Block())

        @block.sync
        def _(eng):
            S.emit("sp", eng, sems)

        @block.tensor
        def _(eng):
            S.emit("pe", eng, sems)

        @block.scalar
        def _(eng):
            S.emit("act", eng, sems)

        @block.vector
        def _(eng):
            S.emit("dve", eng, sems)

        @block.gpsimd
        def _(eng):
            S.emit("pool", eng, sems)

    return nc


def host_consts(core, NB, RS, total_rows):
    RC = NB * 8
    rb = np.zeros((128, NB * 48), np.float32)
    kr_of_p = (np.arange(128) // 64)
    for b in range(1, NB + 1):
        for i in range(4):
            tl = (b - 1) * 4 + i
            r0 = core * RC + 8 * (b - 1) + 2 * i
            olist = [-2, -1, 0, 1, 2]
            if i == 0:
                olist = olist + [3]
            if i == 3:
                olist = [-3] + olist
            for s_, o in enumerate(sorted(olist)):
                for qr in range(2):
                    qrow = r0 + qr
                    seq, r = qrow // RS, qrow % RS
                    rs = min(max(r - 4, 0), RS - 8)
                    krow = r0 + 2 * o + kr_of_p
                    ok = (krow >= 0) & (krow < total_rows) & (krow // RS == seq) & (krow % RS >= rs) & (krow % RS < rs + 8)
                    rb[:, tl * 12 + s_ * 2 + qr] = np.where(ok, 0.0, NEG)
    cm = np.ones((128, 2 * (NB + 2)), np.float32)
    for b in range(1, NB + 1):
        rg = core * RC + 8 * (b - 1)
        if rg % RS == 0:
            cm[:, 2 * b] = 0.0
        if (rg + 8) % RS == 0:
            cm[:, 2 * b + 1] = 0.0
    return rb, cm


def run_layer(xall, RS, w, NB, dbg=0):
    import ml_dtypes
    total_rows = xall.shape[0] // GW
    RC = NB * 8
    assert RC * NCORES == total_rows
    nc = build_program(NB, dbg)
    xpad = np.zeros(((total_rows + 16) * GW, D), np.float32)
    xpad[8 * GW:8 * GW + xall.shape[0]] = xall
    g_mix = np.asarray(w["norm_mix_g"], np.float32).reshape(D)
    g_ffn = np.asarray(w["norm_ffn_g"], np.float32).reshape(D)
    g_fin = np.asarray(w["norm_final_g"], np.float32).reshape(D)
    b_gate = np.asarray(w["b_gate"], np.float32).reshape(2 * D)
    conv_w = np.asarray(w["conv_w"], np.float32).reshape(3, DC)
    rpb = np.asarray(w["rpb"], np.float32).reshape(NH, 15, 31)
    kc_ = np.arange(64)[:, None]
    qc_ = np.arange(64)[None, :]
    idx = np.clip(kc_ - qc_ + 15, 0, 30)
    rpbx = np.ascontiguousarray(np.transpose(rpb[:, :, idx], (2, 0, 1, 3)))
    rpbx = np.concatenate([rpbx, rpbx], axis=0)
    cstart = np.clip(qc_ - 8, 0, GW - 16)
    cv = ((kc_ >= cstart) & (kc_ < cstart + 16)).astype(np.float32)
    cv = np.concatenate([cv, cv], axis=0)
    common = {
        "w_in": np.ascontiguousarray(np.asarray(w["w_in"], np.float32).reshape(D, 5120)),
        "w_ab": np.ascontiguousarray(np.asarray(w["w_attn_branch"], np.float32).reshape(DA, D)),
        "w_cb": np.ascontiguousarray(np.asarray(w["w_conv_branch"], np.float32).reshape(DC, D)),
        "w_out": np.ascontiguousarray(np.asarray(w["w_out"], np.float32).reshape(D, D)),
        "w_fi": np.ascontiguousarray(np.asarray(w["w_ffn_in"], np.float32).reshape(D, 2 * DFF)),
        "w_fd": np.ascontiguousarray(np.asarray(w["w_ffn_down"], np.float32).reshape(DFF, D)),
        "g8": np.ascontiguousarray(g_mix.reshape(8, 128).T),
        "gf8": np.ascontiguousarray(g_ffn.reshape(8, 128).T),
        "gfin": np.ascontiguousarray(np.broadcast_to(g_fin[None, :], (128, D))),
        "bg16": np.ascontiguousarray(b_gate.reshape(16, 128).T),
        "cw": np.ascontiguousarray(conv_w.reshape(3, 4, 128).transpose(2, 1, 0).reshape(128, 12)),
        "rpbx": rpbx,
        "cv": cv,
        "ident": np.eye(128, dtype=np.float32).astype(ml_dtypes.bfloat16),
    }
    in_maps = []
    for c in range(NCORES):
        rb, cm = host_consts(c, NB, RS, total_rows)
        m = dict(common)
        m["x"] = np.ascontiguousarray(xpad[c * RC * GW:(c * RC + RC + 16) * GW])
        m["rb"] = rb
        m["cm"] = cm
        in_maps.append(m)
    res = run_bass_kernel_spmd(nc, in_maps, core_ids=list(range(NCORES)))
    return np.concatenate([np.asarray(r["y"], np.float32) for r in res.results], axis=0)


def kernel(x_prompt, x_sample, norm_mix_g, w_in, b_gate, rpb, conv_w, w_attn_branch, w_conv_branch, w_out,
           norm_ffn_g, w_ffn_in, w_ffn_down, norm_final_g):
    xp = np.asarray(x_prompt, np.float32)
    xs = np.asarray(x_sample, np.float32)
    seq = xp.shape[1]
    RS = seq // GW
    xall = np.concatenate([xp.reshape(-1, D), xs.reshape(-1, D)], axis=0)
    total_rows = xall.shape[0] // GW
    NB = total_rows // NCORES // 8
    w = dict(norm_mix_g=norm_mix_g, w_in=w_in, b_gate=b_gate, rpb=rpb, conv_w=conv_w, w_attn_branch=w_attn_branch,
             w_conv_branch=w_conv_branch, w_out=w_out, norm_ffn_g=norm_ffn_g, w_ffn_in=w_ffn_in, w_ffn_down=w_ffn_down,
             norm_final_g=norm_final_g)
    y = run_layer(xall, RS, w, NB)
    n_p = xp.shape[0] * seq
    return (y[:n_p].reshape(xp.shape), y[n_p:].reshape(xs.shape))
```

```python
import numpy as np
from contextlib import ExitStack
import concourse.bass as bass
import concourse.mybir as mybir
from concourse.bass_utils import run_bass_kernel_spmd

F32 = mybir.dt.float32
BF16 = mybir.dt.bfloat16
AF = mybir.ActivationFunctionType
ALU = mybir.AluOpType

D = 1024
DA = 512
DC = 512
DFF = 2816
NH = 8
GW = 64
NCORES = 8
EPS = 1e-6
NEG = -30000.0
NSLOT = 4


class Res:
    __slots__ = ("name", "w", "r")

    def __init__(self, name):
        self.name = name
        self.w = None
        self.r = {}


class Op:
    __slots__ = ("eng", "fn", "deps", "key", "val", "need", "dma")


class Sched:
    ENGS = ("sp", "pe", "act", "dve", "pool")

    def __init__(self):
        self.q = {e: [] for e in self.ENGS}
        self.dma_cnt = {}

    def op(self, eng, fn, R=(), W=(), dma=None):
        o = Op()
        o.eng, o.fn, o.dma = eng, fn, dma
        o.need = dma is not None
        o.key = dma if dma is not None else eng
        o.val = 0
        if dma is not None:
            self.dma_cnt[dma] = self.dma_cnt.get(dma, 0) + 1
            o.val = 16 * self.dma_cnt[dma]
        deps = {}

        def add(d, raw):
            if d is None or d is o:
                return
            if d.dma is None and o.dma is None and d.eng == eng:
                if eng == "pe" or not raw:
                    return
            deps[id(d)] = d

        for r in R:
            add(r.w, True)
        for r in W:
            add(r.w, False)
            for x in r.r.values():
                add(x, False)
        o.deps = list(deps.values())
        for d in o.deps:
            d.need = True
        for r in R:
            r.r[eng if dma is None else ("d", id(o))] = o
        for r in W:
            r.w = o
            r.r = {}
        self.q[eng].append(o)
        return o

    def barrier_op(self, eng, deps):
        o = Op()
        o.eng, o.fn, o.dma, o.need, o.key, o.val = eng, None, None, False, eng, 0
        o.deps = list(deps)
        for d in o.deps:
            d.need = True
        self.q[eng].append(o)
        return o

    def finalize(self):
        for e in self.ENGS:
            c = 0
            for o in self.q[e]:
                if o.dma is None and o.need and o.fn is not None:
                    c += 1
                    o.val = c

    def emit(self, eng, handle, sems):
        waited = {}
        for o in self.q[eng]:
            for d in o.deps:
                if waited.get(d.key, 0) < d.val:
                    handle.wait_ge(sems[d.key], d.val)
                    waited[d.key] = d.val
            if o.fn is None:
                continue
            ins = o.fn(handle)
            if o.need:
                ins.then_inc(sems[o.key], 16 if o.dma is not None else 1)


def build_program(NB, dbg=0):
    nc = bass.Bass("TRN2", target_bir_lowering=False)
    NBT = NB + 2
    S = Sched()

    def din(name, shape, dt=F32):
        return nc.dram_tensor(name, list(shape), dt, kind="ExternalInput").ap()

    x_d = din("x", [NBT * 512, D])
    w_in_d = din("w_in", [D, 5120])
    w_ab_d = din("w_ab", [DA, D])
    w_cb_d = din("w_cb", [DC, D])
    w_out_d = din("w_out", [D, D])
    w_fi_d = din("w_fi", [D, 2 * DFF])
    w_fd_d = din("w_fd", [DFF, D])
    g8_d = din("g8", [128, 8])
    gf8_d = din("gf8", [128, 8])
    gfin_d = din("gfin", [128, D])
    bg16_d = din("bg16", [128, 16])
    cw_d = din("cw", [128, 12])
    rpbx_d = din("rpbx", [128, NH, 15, 64])
    cv_d = din("cv", [128, 64])
    rb_d = din("rb", [128, NB * 48])
    cm_d = din("cm", [128, 2 * NBT])
    y_d = nc.dram_tensor("y", [NB * 512, D], F32, kind="ExternalOutput").ap()

    s_win = nc.dram_tensor("s_win", [10, 128, 8, 512], BF16).ap()
    s_wbr = nc.dram_tensor("s_wbr", [2, 128, 8, 512], BF16).ap()
    s_wout = nc.dram_tensor("s_wout", [2, 128, 8, 512], BF16).ap()
    s_wfi = nc.dram_tensor("s_wfi", [11, 128, 8, 2, 2, 128], BF16).ap()
    s_wfd = nc.dram_tensor("s_wfd", [3, 2, 128, 8, 512], BF16).ap()

    es = ExitStack()
    with es:
        def sb(name, shape, dt):
            return es.enter_context(nc.sbuf_tensor("sb_" + name, list(shape), dt))

        ring = [sb(f"ring{i}", [128, 4096], BF16) for i in range(NSLOT)]
        ringR = [Res(f"ring{i}") for i in range(NSLOT)]
        er = sb("er", [128, 7, 1024], BF16)
        erR = Res("er")
        rbias = sb("rbias", [128, NB * 48], F32)
        gfin = sb("gfin", [128, D], F32)
        ident = sb("ident", [128, 128], BF16)
        g8 = sb("g8", [128, 8], F32)
        gf8 = sb("gf8", [128, 8], F32)
        bg16 = sb("bg16", [128, 16], F32)
        cw = sb("cw", [128, 12], F32)
        cm = sb("cm", [128, 2 * NBT], F32)
        w0m = sb("w0m", [128, 4, NBT], F32)
        w2m = sb("w2m", [128, 4, NBT], F32)
        cvt = sb("cvt", [128, 64], F32)
        negh = sb("negh", [128, 1], F32)
        constR = Res("const")

        xnT = [sb(f"xnT{i}", [128, 8, 512], BF16) for i in range(2)]
        xnTR = [[Res(f"xnT{i}_{t}") for t in range(4)] for i in range(2)]
        kT = [sb(f"kT{i}", [128, 4, 512], BF16) for i in range(3)]
        kTR = [Res(f"kT{i}") for i in range(3)]
        vv = [sb(f"v{i}", [128, 4, NH, 65], BF16) for i in range(3)]
        vR = [[Res(f"v{i}_{t}") for t in range(4)] for i in range(3)]
        zT = [sb(f"zT{i}", [128, 4, 512], BF16) for i in range(3)]
        zTR = [Res(f"zT{i}") for i in range(3)]

        xin = [sb(f"xin{i}", [128, D], F32) for i in range(2)]
        xinR = [Res(f"xin{i}") for i in range(2)]
        xntm = [sb(f"xntm{i}", [128, D], BF16) for i in range(2)]
        xntmR = [Res(f"xntm{i}") for i in range(2)]
        xw = sb("xw", [128, 4, D], F32)
        xwR = [Res(f"xw{i}") for i in range(4)]
        qT = sb("qT", [128, 4, 2, 512], BF16)
        qTR = Res("qT")
        uT = [sb(f"uT{i}", [128, 512], F32) for i in range(2)]
        uTR = [Res(f"uT{i}") for i in range(2)]
        pexp = [sb(f"pexp{i}", [128, 1024], BF16) for i in range(2)]
        pexpR = [Res(f"pexp{i}") for i in range(2)]
        pT = [sb(f"pT{i}", [128, 1024], BF16) for i in range(2)]
        pTR = [Res(f"pT{i}") for i in range(2)]
        atm = [sb(f"atm{i}", [128, 512], BF16) for i in range(2)]
        atmR = [Res(f"atm{i}") for i in range(2)]
        aT = sb("aT", [128, 4, 512], BF16)
        aTR = Res("aT")
        cvo = [sb(f"cvo{i}", [128, 512], F32) for i in range(2)]
        cvoR = [Res(f"cvo{i}") for i in range(2)]
        cT = sb("cT", [128, 4, 512], BF16)
        cTR = Res("cT")
        ga = [sb(f"ga{i}", [128, 512], F32) for i in range(2)]
        gaR = [Res(f"ga{i}") for i in range(2)]
        gc = [sb(f"gc{i}", [128, 512], F32) for i in range(2)]
        gcR = [Res(f"gc{i}") for i in range(2)]
        mx = sb("mx", [128, 8, 512], BF16)
        mxR = [Res(f"mx{i}") for i in range(4)]
        sg = [sb(f"sg{i}", [128, 512], F32) for i in range(2)]
        sgR = [Res(f"sg{i}") for i in range(2)]
        hT = sb("hT", [128, 8, 512], BF16)
        hTR = Res("hT")
        NST = 8
        st = sb("st", [128, NST, 4], F32)
        stR = [Res(f"st{i}") for i in range(NST)]
        rden = [sb(f"rden{i}", [128, 8], F32) for i in range(2)]
        rdenR = [Res(f"rden{i}") for i in range(2)]
        egh = [sb(f"egh{i}", [128, 15, 64], F32) for i in range(2)]
        eghR = [Res(f"egh{i}") for i in range(2)]

        ps = es.enter_context(nc.psum_tensor("ps", [128, 4096], F32))
        bankR = [Res(f"bank{i}") for i in range(8)]

        def bank(k):
            return ps[:, k * 512:(k + 1) * 512]

        def bank_bf(k):
            return ps[:, k * 512:(k + 1) * 512].bitcast(BF16)

        cnt = {"mm": 0, "tp": 0, "st": 0}

        def next_mm():
            k = cnt["mm"] % 6
            cnt["mm"] += 1
            return k

        def next_tp():
            k = cnt["tp"] % 2
            cnt["tp"] += 1
            return k

        def next_st():
            k = cnt["st"] % NST
            cnt["st"] += 1
            return k

        def cload(dst, src, key):
            S.op("sp", lambda e, d=dst, s=src: e.dma_start(out=d, in_=s), W=[constR], dma=key)

        cload(rbias[:, :], rb_d, "c0")
        cload(gfin[:, :], gfin_d, "c1")
        cload(g8[:, :], g8_d, "c2")
        cload(gf8[:, :], gf8_d, "c3")
        cload(bg16[:, :], bg16_d, "c4")
        cload(cw[:, :], cw_d, "c5")
        cload(cm[:, :], cm_d, "c6")
        cload(cvt[:, :], cv_d, "c7")
        ident_d = din("ident", [128, 128], BF16)
        cload(ident[:, :], ident_d, "c8")
        S.op("pool", lambda e: e.memset(qT[:, :, :, :], 0.0), W=[qTR])
        S.op("pool", lambda e: e.memset(negh[:, :], -0.5), W=[constR])
        for i in range(3):
            S.op("pool", lambda e, i=i: e.memset(vv[i][:, :, :, 64:65], 1.0), W=vR[i])
        cm3 = cm[:, :].rearrange("p (b t) -> p b t", t=2)
        for c in range(4):
            S.op("dve", lambda e, c=c: e.tensor_scalar(out=w0m[:, c, :], in0=cm3[:, :, 0], scalar1=cw[:, c * 3:c * 3 + 1],
                                                        scalar2=None, op0=ALU.mult), R=[constR], W=[constR])
            S.op("dve", lambda e, c=c: e.tensor_scalar(out=w2m[:, c, :], in0=cm3[:, :, 1], scalar1=cw[:, c * 3 + 2:c * 3 + 3],
                                                        scalar2=None, op0=ALU.mult), R=[constR], W=[constR])
        er5 = er[:, :, :].rearrange("p o (h r q) -> p o h r q", h=NH, r=2)
        er_ops = []
        for h in range(NH):
            sl = h % 2
            er_ops.append(lambda h=h, sl=sl: S.op("sp", lambda e: e.dma_start(out=egh[sl][:, :, :], in_=rpbx_d[:, h, :, :]),
                                                  W=[eghR[sl]], dma=f"egh{sl}"))
            er_ops.append(lambda sl=sl: S.op("act", lambda e: e.activation(out=egh[sl][:, :, :], in_=egh[sl][:, :, :],
                                                                          func=AF.Exp),
                                             R=[eghR[sl]], W=[eghR[sl]]))
            for o in range(-3, 4):
                for qr in range(2):
                    for kr in range(2):
                        dr = 2 * o + kr - qr + 7
                        p0, p1 = kr * 64, kr * 64 + 64
                        er_ops.append(lambda o=o, qr=qr, p0=p0, p1=p1, dr=dr, h=h, sl=sl: S.op(
                            "dve", lambda e: e.tensor_tensor(out=er5[p0:p1, o + 3, h, qr, :], in0=egh[sl][p0:p1, dr, :],
                                                             in1=cvt[p0:p1, :], op=ALU.mult),
                            R=[eghR[sl], constR], W=[erR]))

        xw_flat = xw[:, :, :].rearrange("p a b -> p (a b)")
        def f32view(t_):
            return t_[:, :, :].rearrange("p a b -> p (a b)").bitcast(F32)
        prep_in = [xw_flat[:, 0:2048], xw_flat[:, 2048:4096], f32view(xnT[0]), f32view(xnT[1]), f32view(hT), f32view(mx)]
        NPI = len(prep_in)
        prep_inR = [Res(f"pin{i}") for i in range(NPI)]
        NPO = 2 * NSLOT
        prep_out = [ring[i // 2][:, (i % 2) * 2048:(i % 2) * 2048 + 2048] for i in range(NPO)]
        prep_outR = [Res(f"pout{i}") for i in range(NPO)]
        units = []
        for kc in range(8):
            for (c0, n) in ((0, 2048), (2048, 1024)):
                g0, ng = c0 // 512, n // 512
                dst = s_win[g0:g0 + ng, :, kc, :].rearrange("g p w -> p g w")
                units.append((w_in_d[kc * 128:(kc + 1) * 128, c0:c0 + n], n, dst, ("g", ng, 512), g8[:, kc:kc + 1]))
            for u in range(2):
                c0 = 3072 + u * 1024
                dst = s_win[6:10, :, kc, :].rearrange("g p (j u w) -> p g j u w", j=2, u=2)[:, :, :, u, :]
                units.append((w_in_d[kc * 128:(kc + 1) * 128, c0:c0 + 1024], 1024, dst, ("gj", 4, 2, 128), g8[:, kc:kc + 1]))
        for kc in range(4):
            units.append((w_ab_d[kc * 128:(kc + 1) * 128, :], 1024, s_wbr[:, :, kc, :].rearrange("g p w -> p g w"),
                          ("g", 2, 512), None))
            units.append((w_cb_d[kc * 128:(kc + 1) * 128, :], 1024, s_wbr[:, :, 4 + kc, :].rearrange("g p w -> p g w"),
                          ("g", 2, 512), None))
        for kc in range(8):
            dst = s_wout[:, :, kc, :].rearrange("g p w -> p g w")
            units.append((w_out_d[kc * 128:(kc + 1) * 128, :], 1024, dst, ("g", 2, 512), None))
        for kc in range(8):
            for gu in range(2):
                for (j0, nj) in ((0, 16), (16, 6)):
                    c0 = gu * DFF + j0 * 128
                    gi0, ngi = j0 // 2, nj // 2
                    dst = s_wfi[gi0:gi0 + ngi, :, kc, :, gu, :].rearrange("g p j w -> p g j w")
                    units.append((w_fi_d[kc * 128:(kc + 1) * 128, c0:c0 + nj * 128], nj * 128, dst,
                                  ("gj", ngi, 2, 128), gf8[:, kc:kc + 1]))
        for kc in range(22):
            ph, kk = kc // 8, kc % 8
            dst = s_wfd[ph, :, :, kk, :].rearrange("n p w -> p n w")
            units.append((w_fd_d[kc * 128:(kc + 1) * 128, :], 1024, dst, ("g", 2, 512), None))

        prep_last = {}
        ceng = ("dve", "act")
        if dbg == 1:
            units = []
        for ui, (src, n, dst, shp, sc) in enumerate(units):
            si = ui % NPI
            ro = ui % NPO
            S.op("sp", lambda e, si=si, n=n, src=src: e.dma_start(out=prep_in[si][:, 0:n], in_=src),
                 W=[prep_inR[si]], dma=f"pin{si}")
            ce = ceng[ui % 2]
            o_ap = prep_out[ro][:, 0:n]
            i_ap = prep_in[si][:, 0:n]
            if ce == "act":
                if sc is None:
                    fn = lambda e, o_ap=o_ap, i_ap=i_ap: e.activation(out=o_ap, in_=i_ap, func=AF.Copy)
                else:
                    fn = lambda e, o_ap=o_ap, i_ap=i_ap, sc=sc: e.activation(out=o_ap, in_=i_ap, func=AF.Copy, scale=sc)
            else:
                if sc is None:
                    fn = lambda e, o_ap=o_ap, i_ap=i_ap: e.tensor_copy(out=o_ap, in_=i_ap)
                else:
                    fn = lambda e, o_ap=o_ap, i_ap=i_ap, sc=sc: e.tensor_scalar(out=o_ap, in0=i_ap, scalar1=sc,
                                                                                scalar2=None, op0=ALU.mult)
            S.op(ce, fn, R=[prep_inR[si], constR], W=[prep_outR[ro]])
            if shp is None:
                s_ap = o_ap
            elif shp[0] == "g":
                s_ap = o_ap.rearrange("p (g w) -> p g w", g=shp[1])
            else:
                s_ap = o_ap.rearrange("p (g j w) -> p g j w", g=shp[1], j=shp[2])
                prep_last[("u", ro)] = S.op("pool", lambda e, dst=dst, s_ap=s_ap: e.dma_start(out=dst[:, :, 0, :],
                                                                                          in_=s_ap[:, :, 0, :]),
                                            R=[prep_outR[ro]], dma=f"psu{ro}")
                dst = dst[:, :, 1, :]
                s_ap = s_ap[:, :, 1, :]
            prep_last[ro] = S.op("pool", lambda e, dst=dst, s_ap=s_ap: e.dma_start(out=dst, in_=s_ap),
                                 R=[prep_outR[ro]], dma=f"pst{ro}")
            if ui >= 6:
                for _ in range(3):
                    if er_ops:
                        er_ops.pop(0)()
        while er_ops:
            er_ops.pop(0)()
        last_stores = list(prep_last.values())
        for e_ in S.ENGS:
            if last_stores:
                S.barrier_op(e_, last_stores)

        wseq = []

        def win_ap(g):
            return s_win[g].rearrange("p k w -> p (k w)")

        def step_groups(do_a, do_b):
            out = []
            if do_a:
                out += [("win", 1), ("win", 2), ("win", 3), ("win", 5)]
            if do_b:
                out += [("win", 0), ("win", 4), ("wbr", 0), ("win", 6), ("win", 7), ("wbr", 1), ("win", 8), ("win", 9),
                        ("wout", 0), ("wout", 1)]
                for ph in range(3):
                    for gi in range(4 if ph < 2 else 3):
                        out.append(("wfi", ph * 4 + gi))
                    out += [("wfd", ph, 0), ("wfd", ph, 1)]
            return out

        def group_src(gk):
            if gk[0] == "win":
                return win_ap(gk[1])
            if gk[0] == "wbr":
                return s_wbr[gk[1]].rearrange("p k w -> p (k w)")
            if gk[0] == "wout":
                return s_wout[gk[1]].rearrange("p k w -> p (k w)")
            if gk[0] == "wfi":
                return s_wfi[gk[1]].rearrange("p k j u w -> p (k j u w)")
            if gk[0] == "wfd":
                return s_wfd[gk[1], gk[2]].rearrange("p k w -> p (k w)")
            raise ValueError(gk)

        for t in range(NBT):
            wseq += step_groups(True, 1 <= t - 1 <= NB)
        wstate = {"use": 0, "load": 0}

        def wnext(expect, keep=0):
            u = wstate["use"]
            if dbg != 0:
                sl = u % NSLOT
                src = group_src(expect)
                S.op("sp", lambda e, sl=sl, src=src: e.dma_start(out=ring[sl][:, :], in_=src),
                     W=[ringR[sl]], dma=f"ring{sl}")
                wstate["use"] += 1
                return ring[sl], ringR[sl]
            assert wseq[u] == expect, (wseq[u], expect)
            while wstate["load"] < len(wseq) and wstate["load"] < u - keep + NSLOT:
                li = wstate["load"]
                sl = li % NSLOT
                src = group_src(wseq[li])
                S.op("sp", lambda e, sl=sl, src=src: e.dma_start(out=ring[sl][:, :], in_=src),
                     W=[ringR[sl]], dma=f"ring{sl}")
                wstate["load"] += 1
            wstate["use"] += 1
            return ring[u % NSLOT], ringR[u % NSLOT]

        def mm_group(out_ap, bankres, pairs, extraR):
            n = len(pairs)
            last = None
            for k, (l, r) in enumerate(pairs):
                last = S.op("pe", lambda e, l=l, r=r, k=k: e.matmul(out_ap, l, r, start=(k == 0), stop=(k == n - 1)),
                            R=extraR, W=[bankres])
            return last

        def rstd_of(src_ap, srcR, junk_ap, junkR):
            k = next_st()
            S.op("act", lambda e: e.activation(out=junk_ap, in_=src_ap, func=AF.Square, accum_out=st[:, k, 0:1]),
                 R=[srcR], W=[junkR, stR[k]])
            S.op("pool", lambda e: e.tensor_scalar(out=st[:, k, 1:2], in0=st[:, k, 0:1], scalar1=1.0 / D, scalar2=EPS,
                                                   op0=ALU.mult, op1=ALU.add), R=[stR[k]], W=[stR[k]])
            S.op("pool", lambda e: e.tensor_tensor(out=st[:, k, 2:3], in0=st[:, k, 1:2], in1=negh[:, 0:1], op=ALU.pow),
                 R=[stR[k], constR], W=[stR[k]])
            return k

        def transpose_to(src_tm, srcR, nchunk, dst_ap_fn, dstR, evac_eng, tb=None):
            if tb is None:
                tb = next_tp()
            pb = bank_bf(tb)
            for c in range(nchunk):
                S.op("pe", lambda e, c=c: e.transpose(pb[:, c * 128:(c + 1) * 128], src_tm[:, c * 128:(c + 1) * 128],
                                                      ident[:, :]), R=[srcR, constR], W=[bankR[tb]])
            src3 = pb[:, 0:nchunk * 128].rearrange("p (c t) -> p c t", c=nchunk)
            if evac_eng == "act":
                S.op("act", lambda e: e.activation(out=dst_ap_fn(), in_=src3, func=AF.Copy), R=[bankR[tb]], W=[dstR])
            else:
                S.op("dve", lambda e: e.tensor_copy(out=dst_ap_fn(), in_=src3), R=[bankR[tb]], W=[dstR])

        XQ = "pool"

        def preA_ew(b, i):
            xi = (b * 4 + i) % 2
            tok0 = (b * 4 + i) * 128
            S.op(XQ, lambda e, xi=xi, tok0=tok0: e.dma_start(out=xin[xi][:, :], in_=x_d[tok0:tok0 + 128, :]),
                 W=[xinR[xi]], dma=f"xin{xi}")
            k = rstd_of(xin[xi][:, :], xinR[xi], xntm[xi][:, :], xntmR[xi])
            S.op("dve", lambda e, xi=xi, k=k: e.tensor_scalar(out=xntm[xi][:, :], in0=xin[xi][:, :],
                                                             scalar1=st[:, k, 2:3], scalar2=None, op0=ALU.mult),
                 R=[xinR[xi], stR[k]], W=[xntmR[xi]])

        def preA_tr(b, i):
            xs = b % 2
            xi = (b * 4 + i) % 2
            transpose_to(xntm[xi], xntmR[xi], 8, lambda i=i: xnT[xs][:, :, i * 128:(i + 1) * 128], xnTR[xs][i], "dve")

        def matA(b):
            xs = b % 2
            ks = b % 3
            xr = xnTR[xs]
            wt, wr = wnext(("win", 1))
            w3 = wt[:, :].rearrange("p (k w) -> p k w", k=8)
            for hc in range(4):
                bk = next_mm()
                mm_group(bank(bk), bankR[bk], [(w3[:, kc, hc * 128:(hc + 1) * 128], xnT[xs][:, kc, :]) for kc in range(8)],
                         [wr] + xr)
                S.op("act", lambda e, hc=hc, bk=bk: e.activation(out=kT[ks][:, hc, :], in_=bank(bk), func=AF.Copy),
                     R=[bankR[bk]], W=[kTR[ks]])
            wt, wr = wnext(("win", 2))
            w3 = wt[:, :].rearrange("p (k w) -> p k w", k=8)
            for i in range(4):
                bk = next_mm()
                mm_group(bank(bk), bankR[bk], [(xnT[xs][:, kc, i * 128:(i + 1) * 128], w3[:, kc, :]) for kc in range(8)],
                         [wr, xr[i]])
                S.op("dve", lambda e, i=i, bk=bk: e.tensor_copy(out=vv[ks][:, i, :, 0:64],
                                                               in_=bank(bk).rearrange("p (h d) -> p h d", h=NH)),
                     R=[bankR[bk]], W=[vR[ks][i]])
            wt, wr = wnext(("win", 3))
            wu = wt[:, :].rearrange("p (k w) -> p k w", k=8)
            wt2, wr2 = wnext(("win", 5), keep=1)
            wc = wt2[:, :].rearrange("p (k w) -> p k w", k=8)
            for c in range(4):
                us = c % 2
                bk = next_mm()
                mm_group(bank(bk), bankR[bk], [(wu[:, kc, c * 128:(c + 1) * 128], xnT[xs][:, kc, :]) for kc in range(8)],
                         [wr] + xr)
                S.op("act", lambda e, us=us, bk=bk: e.activation(out=uT[us][:, :], in_=bank(bk), func=AF.Copy),
                     R=[bankR[bk]], W=[uTR[us]])
                bk2 = next_mm()
                mm_group(bank(bk2), bankR[bk2], [(wc[:, kc, c * 128:(c + 1) * 128], xnT[xs][:, kc, :]) for kc in range(8)],
                         [wr2] + xr)
                S.op("dve", lambda e, us=us, bk2=bk2, c=c: e.tensor_tensor(out=zT[ks][:, c, :], in0=bank(bk2), in1=uT[us][:, :],
                                                                        op=ALU.mult),
                     R=[bankR[bk2], uTR[us]], W=[zTR[ks]])

        def B_front(b):
            xs = b % 2
            ks = b % 3
            xr = xnTR[xs]
            for i in range(4):
                tok0 = (b * 4 + i) * 128
                S.op(XQ, lambda e, i=i, tok0=tok0: e.dma_start(out=xw[:, i, :], in_=x_d[tok0:tok0 + 128, :]),
                     W=[xwR[i]], dma=f"xw{i}")
            wt, wr = wnext(("win", 0))
            w3 = wt[:, :].rearrange("p (k w) -> p k w", k=8)
            for hc in range(4):
                bk = next_mm()
                mm_group(bank(bk), bankR[bk], [(w3[:, kc, hc * 128:(hc + 1) * 128], xnT[xs][:, kc, :]) for kc in range(8)],
                         [wr] + xr)
                for hh in range(2):
                    S.op("act", lambda e, hc=hc, bk=bk, hh=hh: e.activation(
                        out=qT[hh * 64:hh * 64 + 64, hc, hh, :], in_=bank(bk)[hh * 64:hh * 64 + 64, :], func=AF.Copy, scale=0.125),
                        R=[bankR[bk]], W=[qTR])
        def conv_ew(b, c):
            ks = b % 3
            zp, zn = (b - 1) % 3, (b + 1) % 3
            cs = c % 2
            S.op("dve", lambda e: e.tensor_scalar(out=cvo[cs][:, :], in0=zT[ks][:, c, :],
                                                  scalar1=cw[:, c * 3 + 1:c * 3 + 2], scalar2=None, op0=ALU.mult),
                 R=[zTR[ks], constR], W=[cvoR[cs]])
            S.op("dve", lambda e: e.scalar_tensor_tensor(out=cvo[cs][:, 1:512], in0=zT[ks][:, c, 0:511],
                                                         scalar=cw[:, c * 3:c * 3 + 1], in1=cvo[cs][:, 1:512],
                                                         op0=ALU.mult, op1=ALU.add),
                 R=[zTR[ks], constR, cvoR[cs]], W=[cvoR[cs]])
            S.op("dve", lambda e: e.scalar_tensor_tensor(out=cvo[cs][:, 0:1], in0=zT[zp][:, c, 511:512],
                                                         scalar=w0m[:, c, b:b + 1], in1=cvo[cs][:, 0:1],
                                                         op0=ALU.mult, op1=ALU.add),
                 R=[zTR[zp], constR, cvoR[cs]], W=[cvoR[cs]])
            S.op("dve", lambda e: e.scalar_tensor_tensor(out=cvo[cs][:, 511:512], in0=zT[zn][:, c, 0:1],
                                                         scalar=w2m[:, c, b:b + 1], in1=cvo[cs][:, 511:512],
                                                         op0=ALU.mult, op1=ALU.add),
                 R=[zTR[zn], constR, cvoR[cs]], W=[cvoR[cs]])
            S.op("dve", lambda e: e.scalar_tensor_tensor(out=cT[:, c, 0:511], in0=zT[ks][:, c, 1:512],
                                                         scalar=cw[:, c * 3 + 2:c * 3 + 3], in1=cvo[cs][:, 0:511],
                                                         op0=ALU.mult, op1=ALU.add),
                 R=[zTR[ks], constR, cvoR[cs]], W=[cTR])
            S.op("dve", lambda e: e.tensor_copy(out=cT[:, c, 511:512], in_=cvo[cs][:, 511:512]),
                 R=[cvoR[cs]], W=[cTR])

        def B_bg(b):
            xs = b % 2
            xr = xnTR[xs]
            wt, wr = wnext(("win", 4))
            w3 = wt[:, :].rearrange("p (k w) -> p k w", k=8)
            for c in range(4):
                bk = c % 2
                mm_group(bank(bk), bankR[bk], [(w3[:, kc, c * 128:(c + 1) * 128], xnT[xs][:, kc, :]) for kc in range(8)],
                         [wr] + xr)
                S.op("dve", lambda e, c=c, bk=bk: e.tensor_tensor(out=cT[:, c, :], in0=bank(bk), in1=cT[:, c, :], op=ALU.mult),
                     R=[bankR[bk], cTR], W=[cTR])

        def B_attn(b):
            chunks = []
            for i in range(4):
                olist = [-2, -1, 0, 1, 2]
                if i == 0:
                    olist = olist + [3]
                if i == 3:
                    olist = [-3] + olist
                olist = sorted(olist)
                for s_, o in enumerate(olist):
                    chunks.append((i, s_, o, len(olist)))

            def sbanks_of(n):
                return (2, 3) if n % 2 == 0 else (4, 5)

            def emit_S(n):
                i, s_, o, L = chunks[n]
                g2 = 4 * b + i + o
                b2, i2 = g2 // 4, g2 % 4
                k2 = b2 % 3
                sbk = sbanks_of(n)
                for h in range(NH):
                    for qr in range(2):
                        oap = bank(sbk[qr])[:, h * 64:(h + 1) * 64]
                        S.op("pe", lambda e, oap=oap, h=h, k2=k2, i2=i2, i=i, qr=qr: e.matmul(
                            oap, kT[k2][:, h // 2, i2 * 128:(i2 + 1) * 128],
                            qT[:, h // 2, h % 2, i * 128 + qr * 64:i * 128 + qr * 64 + 64], start=True, stop=True),
                            R=[kTR[k2], qTR], W=[bankR[sbk[qr]]])

            def emit_rest(n):
                i, s_, o, L = chunks[n]
                g2 = 4 * b + i + o
                b2, i2 = g2 // 4, g2 % 4
                k2 = b2 % 3
                sb_ = n % 2
                sbk = sbanks_of(n)
                tl = (b - 1) * 4 + i
                pvb = (6, 7) if i % 2 == 0 else (0, 1)
                pe4 = pexp[sb_][:, :].rearrange("p (h r q) -> p h r q", h=NH, r=2)
                for qr in range(2):
                    col = tl * 12 + s_ * 2 + qr
                    S.op("act", lambda e, pe4=pe4, qr=qr, col=col, bk=sbk[qr]: e.activation(
                        out=pe4[:, :, qr, :], in_=bank(bk).rearrange("p (h q) -> p h q", h=NH), func=AF.Exp,
                        bias=rbias[:, col:col + 1]),
                        R=[bankR[sbk[qr]], constR], W=[pexpR[sb_]])
                S.op("dve", lambda e, sb_=sb_, o=o: e.tensor_tensor(out=pT[sb_][:, :], in0=pexp[sb_][:, :], in1=er[:, o + 3, :],
                                                                   op=ALU.mult),
                     R=[pexpR[sb_], erR], W=[pTR[sb_]])
                for h in range(NH):
                    bk = pvb[h // 4]
                    oap = bank(bk)[:, (h % 4) * 65:(h % 4) * 65 + 65]
                    S.op("pe", lambda e, oap=oap, sb_=sb_, h=h, k2=k2, i2=i2, st_=(s_ == 0 and h % 4 == 0),
                         sp_=(s_ == L - 1): e.matmul(
                        oap, pT[sb_][:, h * 128:(h + 1) * 128], vv[k2][:, i2, h, :], start=st_, stop=sp_,
                        skip_group_check=True),
                        R=[pTR[sb_], vR[k2][i2]], W=[bankR[bk]])
                if s_ == L - 1:
                    ri = i % 2
                    pv4 = ps[:, pvb[0] * 512:pvb[0] * 512 + 1024].rearrange("p (b w) -> p b w", b=2)[:, :, 0:260].rearrange(
                        "p b (h d) -> p b h d", h=4)
                    S.op("dve", lambda e, ri=ri, pv4=pv4: e.reciprocal(out=rden[ri][:, :].rearrange("p (b h) -> p b h", b=2),
                                                                      in_=pv4[:, :, :, 64]),
                         R=[bankR[pvb[0]], bankR[pvb[1]]], W=[rdenR[ri]])
                    for hb in range(2):
                        S.op("dve", lambda e, ri=ri, hb=hb, pv4=pv4: e.tensor_tensor(
                            out=atm[ri][:, hb * 256:(hb + 1) * 256].rearrange("p (h d) -> p h d", h=4),
                            in0=pv4[:, hb, :, 0:64],
                            in1=rden[ri][:, hb * 4:(hb + 1) * 4].unsqueeze(2).to_broadcast([128, 4, 64]), op=ALU.mult),
                            R=[bankR[pvb[hb]], rdenR[ri]], W=[atmR[ri]])
                    deferred.append(lambda i=i, ri=ri, pvb=pvb: transpose_to(
                        atm[ri], atmR[ri], 4, lambda: aT[:, :, i * 128:(i + 1) * 128], aTR, "dve", tb=pvb[0]))

            deferred = []
            emit_S(0)
            for n in range(len(chunks)):
                if n + 1 < len(chunks):
                    emit_S(n + 1)
                if n == 0:
                    B_bg(b)
                pending = list(deferred)
                del deferred[:]
                emit_rest(n)
                for f_ in pending:
                    f_()
            for f_ in deferred:
                f_()


        def B_merge(b):
            xs = b % 2
            xr = xnTR[xs]
            cnt["mm"] = 2
            for half in range(2):
                wbr_t, wbr_r = wnext(("wbr", half))
                wbr3 = wbr_t[:, :].rearrange("p (k w) -> p k w", k=8)
                for k2 in range(2):
                    kq = 2 * half + k2
                    wg_t, wg_r = wnext(("win", 6 + kq), keep=1 + k2)
                    wg = wg_t[:, :].rearrange("p (k j u w) -> p k j u w", k=8, j=2, u=2)
                    for j in range(2):
                        c = 2 * kq + j
                        cc = c % 4
                        gs = c % 2
                        bk = next_mm()
                        mm_group(bank(bk), bankR[bk], [(wg[:, kc, j, 0, :], xnT[xs][:, kc, :]) for kc in range(8)], [wg_r] + xr)
                        S.op("act", lambda e, gs=gs, bk=bk, c=c: e.activation(out=ga[gs][:, :], in_=bank(bk), func=AF.Sigmoid,
                                                                            bias=bg16[:, c:c + 1]),
                             R=[bankR[bk], constR], W=[gaR[gs]])
                        bk = next_mm()
                        mm_group(bank(bk), bankR[bk], [(wg[:, kc, j, 1, :], xnT[xs][:, kc, :]) for kc in range(8)], [wg_r] + xr)
                        S.op("act", lambda e, gs=gs, bk=bk, c=c: e.activation(out=gc[gs][:, :], in_=bank(bk), func=AF.Sigmoid,
                                                                            bias=bg16[:, 8 + c:9 + c]),
                             R=[bankR[bk], constR], W=[gcR[gs]])
                        bk = next_mm()
                        mm_group(bank(bk), bankR[bk], [(wbr3[:, kc, cc * 128:(cc + 1) * 128], aT[:, kc, :]) for kc in range(4)],
                                 [wbr_r, aTR])
                        S.op("dve", lambda e, gs=gs, bk=bk: e.tensor_tensor(out=ga[gs][:, :], in0=bank(bk), in1=ga[gs][:, :],
                                                                           op=ALU.mult),
                             R=[bankR[bk], gaR[gs]], W=[gaR[gs]])
                        bk = next_mm()
                        mm_group(bank(bk), bankR[bk], [(wbr3[:, 4 + kc, cc * 128:(cc + 1) * 128], cT[:, kc, :]) for kc in range(4)],
                                 [wbr_r, cTR])
                        S.op("dve", lambda e, gs=gs, bk=bk: e.tensor_tensor(out=gc[gs][:, :], in0=bank(bk), in1=gc[gs][:, :],
                                                                           op=ALU.mult),
                             R=[bankR[bk], gcR[gs]], W=[gcR[gs]])
                        S.op("pool", lambda e, gs=gs, c=c: e.tensor_tensor(out=mx[:, c, :], in0=ga[gs][:, :], in1=gc[gs][:, :],
                                                                          op=ALU.add),
                             R=[gaR[gs], gcR[gs]], W=mxR)

        def B_wout(b):
            w0t, w0r = wnext(("wout", 0))
            w1t, w1r = wnext(("wout", 1), keep=1)
            wo = [w0t[:, :].rearrange("p (k w) -> p k w", k=8), w1t[:, :].rearrange("p (k w) -> p k w", k=8)]
            wor = [w0r, w1r]
            pend = None
            for i in range(4):
                for n in range(2):
                    bk = next_mm()
                    mm_group(bank(bk), bankR[bk], [(mx[:, kc, i * 128:(i + 1) * 128], wo[n][:, kc, :]) for kc in range(8)],
                             [wor[n], mxR[i]])
                    S.op("dve", lambda e, i=i, n=n, bk=bk: e.tensor_tensor(out=xw[:, i, n * 512:(n + 1) * 512], in0=bank(bk),
                                                                         in1=xw[:, i, n * 512:(n + 1) * 512], op=ALU.add),
                         R=[bankR[bk], xwR[i]], W=[xwR[i]])
                if pend is not None:
                    pend()
                xi = i % 2
                k = rstd_of(xw[:, i, :], xwR[i], xntm[xi][:, :], xntmR[xi])
                S.op("dve", lambda e, xi=xi, k=k, i=i: e.tensor_scalar(out=xntm[xi][:, :], in0=xw[:, i, :],
                                                                      scalar1=st[:, k, 2:3], scalar2=None, op0=ALU.mult),
                     R=[xwR[i], stR[k]], W=[xntmR[xi]])
                pend = (lambda i=i, xi=xi: transpose_to(xntm[xi], xntmR[xi], 8, lambda: mx[:, :, i * 128:(i + 1) * 128],
                                                        mxR[i], "dve"))
            return pend

        def B_ffn(b, hooks):
            stores = []
            for ph in range(3):
                ngi = 4 if ph < 2 else 3
                for gi in range(ngi):
                    wt, wr = wnext(("wfi", ph * 4 + gi))
                    w5 = wt[:, :].rearrange("p (k j u w) -> p k j u w", k=8, j=2, u=2)
                    G = ph * 4 + gi
                    if G in hooks:
                        for f_ in hooks[G][0]:
                            f_()
                    for jj in range(2):
                        jl = gi * 2 + jj
                        ss = jl % 2
                        bk = next_mm()
                        mm_group(bank(bk), bankR[bk], [(w5[:, kc, jj, 0, :], mx[:, kc, :]) for kc in range(8)], [wr] + mxR)
                        S.op("act", lambda e, ss=ss, bk=bk: e.activation(out=sg[ss][:, :], in_=bank(bk), func=AF.Silu),
                             R=[bankR[bk]], W=[sgR[ss]])
                        bk = next_mm()
                        mm_group(bank(bk), bankR[bk], [(w5[:, kc, jj, 1, :], mx[:, kc, :]) for kc in range(8)], [wr] + mxR)
                        S.op("dve", lambda e, ss=ss, bk=bk, jl=jl: e.tensor_tensor(out=hT[:, jl, :], in0=bank(bk), in1=sg[ss][:, :],
                                                                                 op=ALU.mult),
                             R=[bankR[bk], sgR[ss]], W=[hTR])
                    if G in hooks:
                        for f_ in hooks[G][1]:
                            f_()
                nk = 2 * ngi
                w0t, w0r = wnext(("wfd", ph, 0))
                w1t, w1r = wnext(("wfd", ph, 1), keep=1)
                wd = [w0t[:, :].rearrange("p (k w) -> p k w", k=8), w1t[:, :].rearrange("p (k w) -> p k w", k=8)]
                wdr = [w0r, w1r]
                for i in range(4):
                    for n in range(2):
                        bk = next_mm()
                        mm_group(bank(bk), bankR[bk], [(hT[:, kc, i * 128:(i + 1) * 128], wd[n][:, kc, :]) for kc in range(nk)],
                                 [wdr[n], hTR])
                        S.op("dve", lambda e, i=i, n=n, bk=bk: e.tensor_tensor(out=xw[:, i, n * 512:(n + 1) * 512], in0=bank(bk),
                                                                             in1=xw[:, i, n * 512:(n + 1) * 512], op=ALU.add),
                             R=[bankR[bk], xwR[i]], W=[xwR[i]])
                    if ph == 2:
                        xi = i % 2
                        k = rstd_of(xw[:, i, :], xwR[i], xntm[xi][:, :], xntmR[xi])
                        S.op("dve", lambda e, k=k, i=i: e.scalar_tensor_tensor(out=xw[:, i, :], in0=xw[:, i, :], scalar=st[:, k, 2:3],
                                                                              in1=gfin[:, :], op0=ALU.mult, op1=ALU.mult),
                             R=[xwR[i], stR[k], constR], W=[xwR[i]])
                        tok0 = ((b - 1) * 4 + i) * 128
                        stores.append(S.op(XQ, lambda e, i=i, tok0=tok0: e.dma_start(out=y_d[tok0:tok0 + 128, :], in_=xw[:, i, :]),
                                           R=[xwR[i]], dma=f"xw{i}"))
            return stores

        final_stores = []
        for i in range(4):
            preA_ew(0, i)
            preA_tr(0, i)
        for t in range(NBT):
            matA(t)
            hasB = 1 <= t - 1 <= NB
            nxt = t + 1 < NBT
            if hasB:
                for c in range(4):
                    conv_ew(t - 1, c)
                B_front(t - 1)
                B_attn(t - 1)
                B_merge(t - 1)
                pend = B_wout(t - 1)
                pend()
                hooks = {}
                if nxt:
                    ew = [(lambda i=i, t=t: preA_ew(t + 1, i)) for i in range(4)]
                    tr = [(lambda i=i, t=t: preA_tr(t + 1, i)) for i in range(4)]
                    hooks = {0: ([ew[0]], []), 1: ([ew[1]], [tr[0]]), 2: ([ew[2]], [tr[1]]), 3: ([ew[3]], [tr[2]]),
                             4: ([], [tr[3]])}
                final_stores = B_ffn(t - 1, hooks)
            elif nxt:
                for i in range(4):
                    preA_ew(t + 1, i)
                    preA_tr(t + 1, i)
        assert wstate["use"] == len(wseq)
        S.barrier_op("sp", final_stores)
        S.finalize()

        keys = set()
        for e_ in S.ENGS:
            for o in S.q[e_]:
                keys.add(o.key)
        sems = {k: es.enter_context(nc.semaphore(f"s_{k}")) for k in sorted(keys)}
        block = es.enter_context(nc.Block())

        @block.sync
        def _(eng):
            S.emit("sp", eng, sems)

        @block.tensor
        def _(eng):
            S.emit("pe", eng, sems)

        @block.scalar
        def _(eng):
            S.emit("act", eng, sems)

        @block.vector
        def _(eng):
            S.emit("dve", eng, sems)

        @block.gpsimd
        def _(eng):
            S.emit("pool", eng, sems)

    return nc


def host_consts(core, NB, RS, total_rows):
    RC = NB * 8
    rb = np.zeros((128, NB * 48), np.float32)
    kr_of_p = (np.arange(128) // 64)
    for b in range(1, NB + 1):
        for i in range(4):
            tl = (b - 1) * 4 + i
            r0 = core * RC + 8 * (b - 1) + 2 * i
            olist = [-2, -1, 0, 1, 2]
            if i == 0:
                olist = olist + [3]
            if i == 3:
                olist = [-3] + olist
            for s_, o in enumerate(sorted(olist)):
                for qr in range(2):
                    qrow = r0 + qr
                    seq, r = qrow // RS, qrow % RS
                    rs = min(max(r - 4, 0), RS - 8)
                    krow = r0 + 2 * o + kr_of_p
                    ok = (krow >= 0) & (krow < total_rows) & (krow // RS == seq) & (krow % RS >= rs) & (krow % RS < rs + 8)
                    rb[:, tl * 12 + s_ * 2 + qr] = np.where(ok, 0.0, NEG)
    cm = np.ones((128, 2 * (NB + 2)), np.float32)
    for b in range(1, NB + 1):
        rg = core * RC + 8 * (b - 1)
        if rg % RS == 0:
            cm[:, 2 * b] = 0.0
        if (rg + 8) % RS == 0:
            cm[:, 2 * b + 1] = 0.0
    return rb, cm


def run_layer(xall, RS, w, NB, dbg=0):
    import ml_dtypes
    total_rows = xall.shape[0] // GW
    RC = NB * 8
    assert RC * NCORES == total_rows
    nc = build_program(NB, dbg)
    xpad = np.zeros(((total_rows + 16) * GW, D), np.float32)
    xpad[8 * GW:8 * GW + xall.shape[0]] = xall
    g_mix = np.asarray(w["norm_mix_g"], np.float32).reshape(D)
    g_ffn = np.asarray(w["norm_ffn_g"], np.float32).reshape(D)
    g_fin = np.asarray(w["norm_final_g"], np.float32).reshape(D)
    b_gate = np.asarray(w["b_gate"], np.float32).reshape(2 * D)
    conv_w = np.asarray(w["conv_w"], np.float32).reshape(3, DC)
    rpb = np.asarray(w["rpb"], np.float32).reshape(NH, 15, 31)
    kc_ = np.arange(64)[:, None]
    qc_ = np.arange(64)[None, :]
    idx = np.clip(kc_ - qc_ + 15, 0, 30)
    rpbx = np.ascontiguousarray(np.transpose(rpb[:, :, idx], (2, 0, 1, 3)))
    rpbx = np.concatenate([rpbx, rpbx], axis=0)
    cstart = np.clip(qc_ - 8, 0, GW - 16)
    cv = ((kc_ >= cstart) & (kc_ < cstart + 16)).astype(np.float32)
    cv = np.concatenate([cv, cv], axis=0)
    common = {
        "w_in": np.ascontiguousarray(np.asarray(w["w_in"], np.float32).reshape(D, 5120)),
        "w_ab": np.ascontiguousarray(np.asarray(w["w_attn_branch"], np.float32).reshape(DA, D)),
        "w_cb": np.ascontiguousarray(np.asarray(w["w_conv_branch"], np.float32).reshape(DC, D)),
        "w_out": np.ascontiguousarray(np.asarray(w["w_out"], np.float32).reshape(D, D)),
        "w_fi": np.ascontiguousarray(np.asarray(w["w_ffn_in"], np.float32).reshape(D, 2 * DFF)),
        "w_fd": np.ascontiguousarray(np.asarray(w["w_ffn_down"], np.float32).reshape(DFF, D)),
        "g8": np.ascontiguousarray(g_mix.reshape(8, 128).T),
        "gf8": np.ascontiguousarray(g_ffn.reshape(8, 128).T),
        "gfin": np.ascontiguousarray(np.broadcast_to(g_fin[None, :], (128, D))),
        "bg16": np.ascontiguousarray(b_gate.reshape(16, 128).T),
        "cw": np.ascontiguousarray(conv_w.reshape(3, 4, 128).transpose(2, 1, 0).reshape(128, 12)),
        "rpbx": rpbx,
        "cv": cv,
        "ident": np.eye(128, dtype=np.float32).astype(ml_dtypes.bfloat16),
    }
    in_maps = []
    for c in range(NCORES):
        rb, cm = host_consts(c, NB, RS, total_rows)
        m = dict(common)
        m["x"] = np.ascontiguousarray(xpad[c * RC * GW:(c * RC + RC + 16) * GW])
        m["rb"] = rb
        m["cm"] = cm
        in_maps.append(m)
    res = run_bass_kernel_spmd(nc, in_maps, core_ids=list(range(NCORES)))
    return np.concatenate([np.asarray(r["y"], np.float32) for r in res.results], axis=0)


def kernel(x_prompt, x_sample, norm_mix_g, w_in, b_gate, rpb, conv_w, w_attn_branch, w_conv_branch, w_out,
           norm_ffn_g, w_ffn_in, w_ffn_down, norm_final_g):
    xp = np.asarray(x_prompt, np.float32)
    xs = np.asarray(x_sample, np.float32)
    seq = xp.shape[1]
    RS = seq // GW
    xall = np.concatenate([xp.reshape(-1, D), xs.reshape(-1, D)], axis=0)
    total_rows = xall.shape[0] // GW
    NB = total_rows // NCORES // 8
    w = dict(norm_mix_g=norm_mix_g, w_in=w_in, b_gate=b_gate, rpb=rpb, conv_w=conv_w, w_attn_branch=w_attn_branch,
             w_conv_branch=w_conv_branch, w_out=w_out, norm_ffn_g=norm_ffn_g, w_ffn_in=w_ffn_in, w_ffn_down=w_ffn_down,
             norm_final_g=norm_final_g)
    y = run_layer(xall, RS, w, NB)
    n_p = xp.shape[0] * seq
    return (y[:n_p].reshape(xp.shape), y[n_p:].reshape(xs.shape))
```

```python
import numpy as np
from contextlib import ExitStack
import concourse.bass as bass
import concourse.mybir as mybir
from concourse.bass_utils import run_bass_kernel_spmd

F32 = mybir.dt.float32
BF16 = mybir.dt.bfloat16
AF = mybir.ActivationFunctionType
ALU = mybir.AluOpType

D = 1024
DA = 512
DC = 512
DFF = 2816
NH = 8
GW = 64
NCORES = 8
EPS = 1e-6
NEG = -30000.0
NSLOT = 4


class Res:
    __slots__ = ("name", "w", "r")

    def __init__(self, name):
        self.name = name
        self.w = None
        self.r = {}


class Op:
    __slots__ = ("eng", "fn", "deps", "key", "val", "need", "dma")


class Sched:
    ENGS = ("sp", "pe", "act", "dve", "pool")

    def __init__(self):
        self.q = {e: [] for e in self.ENGS}
        self.dma_cnt = {}

    def op(self, eng, fn, R=(), W=(), dma=None):
        o = Op()
        o.eng, o.fn, o.dma = eng, fn, dma
        o.need = dma is not None
        o.key = dma if dma is not None else eng
        o.val = 0
        if dma is not None:
            self.dma_cnt[dma] = self.dma_cnt.get(dma, 0) + 1
            o.val = 16 * self.dma_cnt[dma]
        deps = {}

        def add(d, raw):
            if d is None or d is o:
                return
            if d.dma is None and o.dma is None and d.eng == eng:
                if eng == "pe" or not raw:
                    return
            deps[id(d)] = d

        for r in R:
            add(r.w, True)
        for r in W:
            add(r.w, False)
            for x in r.r.values():
                add(x, False)
        o.deps = list(deps.values())
        for d in o.deps:
            d.need = True
        for r in R:
            r.r[eng if dma is None else ("d", id(o))] = o
        for r in W:
            r.w = o
            r.r = {}
        self.q[eng].append(o)
        return o

    def barrier_op(self, eng, deps):
        o = Op()
        o.eng, o.fn, o.dma, o.need, o.key, o.val = eng, None, None, False, eng, 0
        o.deps = list(deps)
        for d in o.deps:
            d.need = True
        self.q[eng].append(o)
        return o

    def finalize(self):
        for e in self.ENGS:
            c = 0
            for o in self.q[e]:
                if o.dma is None and o.need and o.fn is not None:
                    c += 1
                    o.val = c

    def emit(self, eng, handle, sems):
        waited = {}
        for o in self.q[eng]:
            for d in o.deps:
                if waited.get(d.key, 0) < d.val:
                    handle.wait_ge(sems[d.key], d.val)
                    waited[d.key] = d.val
            if o.fn is None:
                continue
            ins = o.fn(handle)
            if o.need:
                ins.then_inc(sems[o.key], 16 if o.dma is not None else 1)


def build_program(NB, dbg=0):
    nc = bass.Bass("TRN2", target_bir_lowering=False)
    NBT = NB + 2
    S = Sched()

    def din(name, shape, dt=F32):
        return nc.dram_tensor(name, list(shape), dt, kind="ExternalInput").ap()

    x_d = din("x", [NBT * 512, D])
    w_in_d = din("w_in", [D, 5120])
    w_ab_d = din("w_ab", [DA, D])
    w_cb_d = din("w_cb", [DC, D])
    w_out_d = din("w_out", [D, D])
    w_fi_d = din("w_fi", [D, 2 * DFF])
    w_fd_d = din("w_fd", [DFF, D])
    g8_d = din("g8", [128, 8])
    gf8_d = din("gf8", [128, 8])
    gfin_d = din("gfin", [128, D])
    bg16_d = din("bg16", [128, 16])
    cw_d = din("cw", [128, 12])
    rpbx_d = din("rpbx", [128, NH, 15, 64])
    cv_d = din("cv", [128, 64])
    rb_d = din("rb", [128, NB * 48])
    cm_d = din("cm", [128, 2 * NBT])
    y_d = nc.dram_tensor("y", [NB * 512, D], F32, kind="ExternalOutput").ap()

    s_win = nc.dram_tensor("s_win", [10, 128, 8, 512], BF16).ap()
    s_wbr = nc.dram_tensor("s_wbr", [2, 128, 8, 512], BF16).ap()
    s_wout = nc.dram_tensor("s_wout", [2, 128, 8, 512], BF16).ap()
    s_wfi = nc.dram_tensor("s_wfi", [11, 128, 8, 2, 2, 128], BF16).ap()
    s_wfd = nc.dram_tensor("s_wfd", [3, 2, 128, 8, 512], BF16).ap()

    es = ExitStack()
    with es:
        def sb(name, shape, dt):
            return es.enter_context(nc.sbuf_tensor("sb_" + name, list(shape), dt))

        ring = [sb(f"ring{i}", [128, 4096], BF16) for i in range(NSLOT)]
        ringR = [Res(f"ring{i}") for i in range(NSLOT)]
        er = sb("er", [128, 7, 1024], BF16)
        erR = Res("er")
        rbias = sb("rbias", [128, NB * 48], F32)
        gfin = sb("gfin", [128, D], F32)
        ident = sb("ident", [128, 128], BF16)
        g8 = sb("g8", [128, 8], F32)
        gf8 = sb("gf8", [128, 8], F32)
        bg16 = sb("bg16", [128, 16], F32)
        cw = sb("cw", [128, 12], F32)
        cm = sb("cm", [128, 2 * NBT], F32)
        w0m = sb("w0m", [128, 4, NBT], F32)
        w2m = sb("w2m", [128, 4, NBT], F32)
        cvt = sb("cvt", [128, 64], F32)
        negh = sb("negh", [128, 1], F32)
        constR = Res("const")

        xnT = [sb(f"xnT{i}", [128, 8, 512], BF16) for i in range(2)]
        xnTR = [[Res(f"xnT{i}_{t}") for t in range(4)] for i in range(2)]
        kT = [sb(f"kT{i}", [128, 4, 512], BF16) for i in range(3)]
        kTR = [Res(f"kT{i}") for i in range(3)]
        vv = [sb(f"v{i}", [128, 4, NH, 65], BF16) for i in range(3)]
        vR = [[Res(f"v{i}_{t}") for t in range(4)] for i in range(3)]
        zT = [sb(f"zT{i}", [128, 4, 512], BF16) for i in range(3)]
        zTR = [Res(f"zT{i}") for i in range(3)]

        xin = [sb(f"xin{i}", [128, D], F32) for i in range(2)]
        xinR = [Res(f"xin{i}") for i in range(2)]
        xntm = [sb(f"xntm{i}", [128, D], BF16) for i in range(2)]
        xntmR = [Res(f"xntm{i}") for i in range(2)]
        xw = sb("xw", [128, 4, D], F32)
        xwR = [Res(f"xw{i}") for i in range(4)]
        qT = sb("qT", [128, 4, 2, 512], BF16)
        qTR = Res("qT")
        uT = [sb(f"uT{i}", [128, 512], F32) for i in range(2)]
        uTR = [Res(f"uT{i}") for i in range(2)]
        pexp = [sb(f"pexp{i}", [128, 1024], BF16) for i in range(2)]
        pexpR = [Res(f"pexp{i}") for i in range(2)]
        pT = [sb(f"pT{i}", [128, 1024], BF16) for i in range(2)]
        pTR = [Res(f"pT{i}") for i in range(2)]
        atm = [sb(f"atm{i}", [128, 512], BF16) for i in range(2)]
        atmR = [Res(f"atm{i}") for i in range(2)]
        aT = sb("aT", [128, 4, 512], BF16)
        aTR = Res("aT")
        cvo = [sb(f"cvo{i}", [128, 512], F32) for i in range(2)]
        cvoR = [Res(f"cvo{i}") for i in range(2)]
        cT = sb("cT", [128, 4, 512], BF16)
        cTR = Res("cT")
        ga = [sb(f"ga{i}", [128, 512], F32) for i in range(2)]
        gaR = [Res(f"ga{i}") for i in range(2)]
        gc = [sb(f"gc{i}", [128, 512], F32) for i in range(2)]
        gcR = [Res(f"gc{i}") for i in range(2)]
        mx = sb("mx", [128, 8, 512], BF16)
        mxR = [Res(f"mx{i}") for i in range(4)]
        sg = [sb(f"sg{i}", [128, 512], F32) for i in range(2)]
        sgR = [Res(f"sg{i}") for i in range(2)]
        hT = sb("hT", [128, 8, 512], BF16)
        hTR = Res("hT")
        NST = 8
        st = sb("st", [128, NST, 4], F32)
        stR = [Res(f"st{i}") for i in range(NST)]
        rden = [sb(f"rden{i}", [128, 8], F32) for i in range(2)]
        rdenR = [Res(f"rden{i}") for i in range(2)]
        egh = [sb(f"egh{i}", [128, 15, 64], F32) for i in range(2)]
        eghR = [Res(f"egh{i}") for i in range(2)]

        ps = es.enter_context(nc.psum_tensor("ps", [128, 4096], F32))
        bankR = [Res(f"bank{i}") for i in range(8)]

        def bank(k):
            return ps[:, k * 512:(k + 1) * 512]

        def bank_bf(k):
            return ps[:, k * 512:(k + 1) * 512].bitcast(BF16)

        cnt = {"mm": 0, "tp": 0, "st": 0}

        def next_mm():
            k = cnt["mm"] % 6
            cnt["mm"] += 1
            return k

        def next_tp():
            k = cnt["tp"] % 2
            cnt["tp"] += 1
            return k

        def next_st():
            k = cnt["st"] % NST
            cnt["st"] += 1
            return k

        def cload(dst, src, key):
            S.op("sp", lambda e, d=dst, s=src: e.dma_start(out=d, in_=s), W=[constR], dma=key)

        cload(rbias[:, :], rb_d, "c0")
        cload(gfin[:, :], gfin_d, "c1")
        cload(g8[:, :], g8_d, "c2")
        cload(gf8[:, :], gf8_d, "c3")
        cload(bg16[:, :], bg16_d, "c4")
        cload(cw[:, :], cw_d, "c5")
        cload(cm[:, :], cm_d, "c6")
        cload(cvt[:, :], cv_d, "c7")
        ident_d = din("ident", [128, 128], BF16)
        cload(ident[:, :], ident_d, "c8")
        S.op("pool", lambda e: e.memset(qT[:, :, :, :], 0.0), W=[qTR])
        S.op("pool", lambda e: e.memset(negh[:, :], -0.5), W=[constR])
        for i in range(3):
            S.op("pool", lambda e, i=i: e.memset(vv[i][:, :, :, 64:65], 1.0), W=vR[i])
        cm3 = cm[:, :].rearrange("p (b t) -> p b t", t=2)
        for c in range(4):
            S.op("dve", lambda e, c=c: e.tensor_scalar(out=w0m[:, c, :], in0=cm3[:, :, 0], scalar1=cw[:, c * 3:c * 3 + 1],
                                                        scalar2=None, op0=ALU.mult), R=[constR], W=[constR])
            S.op("dve", lambda e, c=c: e.tensor_scalar(out=w2m[:, c, :], in0=cm3[:, :, 1], scalar1=cw[:, c * 3 + 2:c * 3 + 3],
                                                        scalar2=None, op0=ALU.mult), R=[constR], W=[constR])
        er5 = er[:, :, :].rearrange("p o (h r q) -> p o h r q", h=NH, r=2)
        er_ops = []
        for h in range(NH):
            sl = h % 2
            er_ops.append(lambda h=h, sl=sl: S.op("sp", lambda e: e.dma_start(out=egh[sl][:, :, :], in_=rpbx_d[:, h, :, :]),
                                                  W=[eghR[sl]], dma=f"egh{sl}"))
            er_ops.append(lambda sl=sl: S.op("act", lambda e: e.activation(out=egh[sl][:, :, :], in_=egh[sl][:, :, :],
                                                                          func=AF.Exp),
                                             R=[eghR[sl]], W=[eghR[sl]]))
            for o in range(-3, 4):
                for qr in range(2):
                    for kr in range(2):
                        dr = 2 * o + kr - qr + 7
                        p0, p1 = kr * 64, kr * 64 + 64
                        er_ops.append(lambda o=o, qr=qr, p0=p0, p1=p1, dr=dr, h=h, sl=sl: S.op(
                            "dve", lambda e: e.tensor_tensor(out=er5[p0:p1, o + 3, h, qr, :], in0=egh[sl][p0:p1, dr, :],
                                                             in1=cvt[p0:p1, :], op=ALU.mult),
                            R=[eghR[sl], constR], W=[erR]))

        xw_flat = xw[:, :, :].rearrange("p a b -> p (a b)")
        def f32view(t_):
            return t_[:, :, :].rearrange("p a b -> p (a b)").bitcast(F32)
        prep_in = [xw_flat[:, 0:2048], xw_flat[:, 2048:4096], f32view(xnT[0]), f32view(xnT[1]), f32view(hT), f32view(mx)]
        NPI = len(prep_in)
        prep_inR = [Res(f"pin{i}") for i in range(NPI)]
        NPO = 2 * NSLOT
        prep_out = [ring[i // 2][:, (i % 2) * 2048:(i % 2) * 2048 + 2048] for i in range(NPO)]
        prep_outR = [Res(f"pout{i}") for i in range(NPO)]
        units = []
        for kc in range(8):
            for (c0, n) in ((0, 2048), (2048, 1024)):
                g0, ng = c0 // 512, n // 512
                dst = s_win[g0:g0 + ng, :, kc, :].rearrange("g p w -> p g w")
                units.append((w_in_d[kc * 128:(kc + 1) * 128, c0:c0 + n], n, dst, ("g", ng, 512), g8[:, kc:kc + 1]))
            for u in range(2):
                c0 = 3072 + u * 1024
                dst = s_win[6:10, :, kc, :].rearrange("g p (j u w) -> p g j u w", j=2, u=2)[:, :, :, u, :]
                units.append((w_in_d[kc * 128:(kc + 1) * 128, c0:c0 + 1024], 1024, dst, ("gj", 4, 2, 128), g8[:, kc:kc + 1]))
        for kc in range(4):
            units.append((w_ab_d[kc * 128:(kc + 1) * 128, :], 1024, s_wbr[:, :, kc, :].rearrange("g p w -> p g w"),
                          ("g", 2, 512), None))
            units.append((w_cb_d[kc * 128:(kc + 1) * 128, :], 1024, s_wbr[:, :, 4 + kc, :].rearrange("g p w -> p g w"),
                          ("g", 2, 512), None))
        for kc in range(8):
            dst = s_wout[:, :, kc, :].rearrange("g p w -> p g w")
            units.append((w_out_d[kc * 128:(kc + 1) * 128, :], 1024, dst, ("g", 2, 512), None))
        for kc in range(8):
            for gu in range(2):
                for (j0, nj) in ((0, 16), (16, 6)):
                    c0 = gu * DFF + j0 * 128
                    gi0, ngi = j0 // 2, nj // 2
                    dst = s_wfi[gi0:gi0 + ngi, :, kc, :, gu, :].rearrange("g p j w -> p g j w")
                    units.append((w_fi_d[kc * 128:(kc + 1) * 128, c0:c0 + nj * 128], nj * 128, dst,
                                  ("gj", ngi, 2, 128), gf8[:, kc:kc + 1]))
        for kc in range(22):
            ph, kk = kc // 8, kc % 8
            dst = s_wfd[ph, :, :, kk, :].rearrange("n p w -> p n w")
            units.append((w_fd_d[kc * 128:(kc + 1) * 128, :], 1024, dst, ("g", 2, 512), None))

        prep_last = {}
        ceng = ("dve", "act")
        if dbg == 1:
            units = []
        for ui, (src, n, dst, shp, sc) in enumerate(units):
            si = ui % NPI
            ro = ui % NPO
            S.op("sp", lambda e, si=si, n=n, src=src: e.dma_start(out=prep_in[si][:, 0:n], in_=src),
                 W=[prep_inR[si]], dma=f"pin{si}")
            ce = ceng[ui % 2]
            o_ap = prep_out[ro][:, 0:n]
            i_ap = prep_in[si][:, 0:n]
            if ce == "act":
                if sc is None:
                    fn = lambda e, o_ap=o_ap, i_ap=i_ap: e.activation(out=o_ap, in_=i_ap, func=AF.Copy)
                else:
                    fn = lambda e, o_ap=o_ap, i_ap=i_ap, sc=sc: e.activation(out=o_ap, in_=i_ap, func=AF.Copy, scale=sc)
            else:
                if sc is None:
                    fn = lambda e, o_ap=o_ap, i_ap=i_ap: e.tensor_copy(out=o_ap, in_=i_ap)
                else:
                    fn = lambda e, o_ap=o_ap, i_ap=i_ap, sc=sc: e.tensor_scalar(out=o_ap, in0=i_ap, scalar1=sc,
                                                                                scalar2=None, op0=ALU.mult)
            S.op(ce, fn, R=[prep_inR[si], constR], W=[prep_outR[ro]])
            if shp is None:
                s_ap = o_ap
            elif shp[0] == "g":
                s_ap = o_ap.rearrange("p (g w) -> p g w", g=shp[1])
            else:
                s_ap = o_ap.rearrange("p (g j w) -> p g j w", g=shp[1], j=shp[2])
                prep_last[("u", ro)] = S.op("pool", lambda e, dst=dst, s_ap=s_ap: e.dma_start(out=dst[:, :, 0, :],
                                                                                          in_=s_ap[:, :, 0, :]),
                                            R=[prep_outR[ro]], dma=f"psu{ro}")
                dst = dst[:, :, 1, :]
                s_ap = s_ap[:, :, 1, :]
            prep_last[ro] = S.op("pool", lambda e, dst=dst, s_ap=s_ap: e.dma_start(out=dst, in_=s_ap),
                                 R=[prep_outR[ro]], dma=f"pst{ro}")
            if ui >= 6:
                for _ in range(3):
                    if er_ops:
                        er_ops.pop(0)()
        while er_ops:
            er_ops.pop(0)()
        last_stores = list(prep_last.values())
        for e_ in S.ENGS:
            if last_stores:
                S.barrier_op(e_, last_stores)

        wseq = []

        def win_ap(g):
            return s_win[g].rearrange("p k w -> p (k w)")

        def step_groups(do_a, do_b):
            out = []
            if do_a:
                out += [("win", 1), ("win", 2), ("win", 3), ("win", 5)]
            if do_b:
                out += [("win", 0), ("win", 4), ("wbr", 0), ("win", 6), ("win", 7), ("wbr", 1), ("win", 8), ("win", 9),
                        ("wout", 0), ("wout", 1)]
                for ph in range(3):
                    for gi in range(4 if ph < 2 else 3):
                        out.append(("wfi", ph * 4 + gi))
                    out += [("wfd", ph, 0), ("wfd", ph, 1)]
            return out

        def group_src(gk):
            if gk[0] == "win":
                return win_ap(gk[1])
            if gk[0] == "wbr":
                return s_wbr[gk[1]].rearrange("p k w -> p (k w)")
            if gk[0] == "wout":
                return s_wout[gk[1]].rearrange("p k w -> p (k w)")
            if gk[0] == "wfi":
                return s_wfi[gk[1]].rearrange("p k j u w -> p (k j u w)")
            if gk[0] == "wfd":
                nk_ = 8 if gk[1] < 2 else 6
                return s_wfd[gk[1], gk[2]][:, 0:nk_, :].rearrange("p k w -> p (k w)")
            raise ValueError(gk)

        for t in range(NBT):
            wseq += step_groups(True, 1 <= t - 1 <= NB)
        wstate = {"use": 0, "load": 0}

        def wnext(expect, keep=0):
            u = wstate["use"]
            if dbg != 0:
                sl = u % NSLOT
                src = group_src(expect)
                S.op("sp", lambda e, sl=sl, src=src: e.dma_start(out=ring[sl][:, :], in_=src),
                     W=[ringR[sl]], dma=f"ring{sl}")
                wstate["use"] += 1
                return ring[sl], ringR[sl]
            assert wseq[u] == expect, (wseq[u], expect)
            while wstate["load"] < len(wseq) and wstate["load"] < u - keep + NSLOT:
                li = wstate["load"]
                sl = li % NSLOT
                src = group_src(wseq[li])
                S.op("sp", lambda e, sl=sl, src=src: e.dma_start(out=ring[sl][:, 0:src.shape[1]], in_=src),
                     W=[ringR[sl]], dma=f"ring{sl}")
                wstate["load"] += 1
            wstate["use"] += 1
            return ring[u % NSLOT], ringR[u % NSLOT]

        def mm_group(out_ap, bankres, pairs, extraR):
            n = len(pairs)
            last = None
            for k, (l, r) in enumerate(pairs):
                last = S.op("pe", lambda e, l=l, r=r, k=k: e.matmul(out_ap, l, r, start=(k == 0), stop=(k == n - 1)),
                            R=extraR, W=[bankres])
            return last

        def rstd_of(src_ap, srcR, junk_ap, junkR):
            k = next_st()
            S.op("act", lambda e: e.activation(out=junk_ap, in_=src_ap, func=AF.Square, accum_out=st[:, k, 0:1]),
                 R=[srcR], W=[junkR, stR[k]])
            S.op("pool", lambda e: e.tensor_scalar(out=st[:, k, 1:2], in0=st[:, k, 0:1], scalar1=1.0 / D, scalar2=EPS,
                                                   op0=ALU.mult, op1=ALU.add), R=[stR[k]], W=[stR[k]])
            S.op("pool", lambda e: e.tensor_tensor(out=st[:, k, 2:3], in0=st[:, k, 1:2], in1=negh[:, 0:1], op=ALU.pow),
                 R=[stR[k], constR], W=[stR[k]])
            return k

        def transpose_to(src_tm, srcR, nchunk, dst_ap_fn, dstR, evac_eng, tb=None):
            if tb is None:
                tb = next_tp()
            pb = bank_bf(tb)
            for c in range(nchunk):
                S.op("pe", lambda e, c=c: e.transpose(pb[:, c * 128:(c + 1) * 128], src_tm[:, c * 128:(c + 1) * 128],
                                                      ident[:, :]), R=[srcR, constR], W=[bankR[tb]])
            src3 = pb[:, 0:nchunk * 128].rearrange("p (c t) -> p c t", c=nchunk)
            if evac_eng == "act":
                S.op("act", lambda e: e.activation(out=dst_ap_fn(), in_=src3, func=AF.Copy), R=[bankR[tb]], W=[dstR])
            else:
                S.op("dve", lambda e: e.tensor_copy(out=dst_ap_fn(), in_=src3), R=[bankR[tb]], W=[dstR])

        XQ = "pool"

        def preA_ew(b, i):
            xi = (b * 4 + i) % 2
            tok0 = (b * 4 + i) * 128
            S.op(XQ, lambda e, xi=xi, tok0=tok0: e.dma_start(out=xin[xi][:, :], in_=x_d[tok0:tok0 + 128, :]),
                 W=[xinR[xi]], dma=f"xin{xi}")
            k = rstd_of(xin[xi][:, :], xinR[xi], xntm[xi][:, :], xntmR[xi])
            S.op("dve", lambda e, xi=xi, k=k: e.tensor_scalar(out=xntm[xi][:, :], in0=xin[xi][:, :],
                                                             scalar1=st[:, k, 2:3], scalar2=None, op0=ALU.mult),
                 R=[xinR[xi], stR[k]], W=[xntmR[xi]])

        def preA_tr(b, i):
            xs = b % 2
            xi = (b * 4 + i) % 2
            transpose_to(xntm[xi], xntmR[xi], 8, lambda i=i: xnT[xs][:, :, i * 128:(i + 1) * 128], xnTR[xs][i], "dve")

        def matA(b):
            xs = b % 2
            ks = b % 3
            xr = xnTR[xs]
            wt, wr = wnext(("win", 1))
            w3 = wt[:, :].rearrange("p (k w) -> p k w", k=8)
            for hc in range(4):
                bk = next_mm()
                mm_group(bank(bk), bankR[bk], [(w3[:, kc, hc * 128:(hc + 1) * 128], xnT[xs][:, kc, :]) for kc in range(8)],
                         [wr] + xr)
                S.op("act", lambda e, hc=hc, bk=bk: e.activation(out=kT[ks][:, hc, :], in_=bank(bk), func=AF.Copy),
                     R=[bankR[bk]], W=[kTR[ks]])
            wt, wr = wnext(("win", 2))
            w3 = wt[:, :].rearrange("p (k w) -> p k w", k=8)
            for i in range(4):
                bk = next_mm()
                mm_group(bank(bk), bankR[bk], [(xnT[xs][:, kc, i * 128:(i + 1) * 128], w3[:, kc, :]) for kc in range(8)],
                         [wr, xr[i]])
                S.op("dve", lambda e, i=i, bk=bk: e.tensor_copy(out=vv[ks][:, i, :, 0:64],
                                                               in_=bank(bk).rearrange("p (h d) -> p h d", h=NH)),
                     R=[bankR[bk]], W=[vR[ks][i]])
            wt, wr = wnext(("win", 3))
            wu = wt[:, :].rearrange("p (k w) -> p k w", k=8)
            wt2, wr2 = wnext(("win", 5), keep=1)
            wc = wt2[:, :].rearrange("p (k w) -> p k w", k=8)
            for c in range(4):
                us = c % 2
                bk = next_mm()
                mm_group(bank(bk), bankR[bk], [(wu[:, kc, c * 128:(c + 1) * 128], xnT[xs][:, kc, :]) for kc in range(8)],
                         [wr] + xr)
                S.op("act", lambda e, us=us, bk=bk: e.activation(out=uT[us][:, :], in_=bank(bk), func=AF.Copy),
                     R=[bankR[bk]], W=[uTR[us]])
                bk2 = next_mm()
                mm_group(bank(bk2), bankR[bk2], [(wc[:, kc, c * 128:(c + 1) * 128], xnT[xs][:, kc, :]) for kc in range(8)],
                         [wr2] + xr)
                S.op("dve", lambda e, us=us, bk2=bk2, c=c: e.tensor_tensor(out=zT[ks][:, c, :], in0=bank(bk2), in1=uT[us][:, :],
                                                                        op=ALU.mult),
                     R=[bankR[bk2], uTR[us]], W=[zTR[ks]])

        def B_front(b):
            xs = b % 2
            ks = b % 3
            xr = xnTR[xs]
            for i in range(4):
                tok0 = (b * 4 + i) * 128
                S.op(XQ, lambda e, i=i, tok0=tok0: e.dma_start(out=xw[:, i, :], in_=x_d[tok0:tok0 + 128, :]),
                     W=[xwR[i]], dma=f"xw{i}")
            wt, wr = wnext(("win", 0))
            w3 = wt[:, :].rearrange("p (k w) -> p k w", k=8)
            for hc in range(4):
                bk = next_mm()
                mm_group(bank(bk), bankR[bk], [(w3[:, kc, hc * 128:(hc + 1) * 128], xnT[xs][:, kc, :]) for kc in range(8)],
                         [wr] + xr)
                for hh in range(2):
                    S.op("act", lambda e, hc=hc, bk=bk, hh=hh: e.activation(
                        out=qT[hh * 64:hh * 64 + 64, hc, hh, :], in_=bank(bk)[hh * 64:hh * 64 + 64, :], func=AF.Copy, scale=0.125),
                        R=[bankR[bk]], W=[qTR])
        def conv_ew(b, c):
            ks = b % 3
            zp, zn = (b - 1) % 3, (b + 1) % 3
            cs = c % 2
            S.op("dve", lambda e: e.tensor_scalar(out=cvo[cs][:, :], in0=zT[ks][:, c, :],
                                                  scalar1=cw[:, c * 3 + 1:c * 3 + 2], scalar2=None, op0=ALU.mult),
                 R=[zTR[ks], constR], W=[cvoR[cs]])
            S.op("dve", lambda e: e.scalar_tensor_tensor(out=cvo[cs][:, 1:512], in0=zT[ks][:, c, 0:511],
                                                         scalar=cw[:, c * 3:c * 3 + 1], in1=cvo[cs][:, 1:512],
                                                         op0=ALU.mult, op1=ALU.add),
                 R=[zTR[ks], constR, cvoR[cs]], W=[cvoR[cs]])
            S.op("dve", lambda e: e.scalar_tensor_tensor(out=cvo[cs][:, 0:1], in0=zT[zp][:, c, 511:512],
                                                         scalar=w0m[:, c, b:b + 1], in1=cvo[cs][:, 0:1],
                                                         op0=ALU.mult, op1=ALU.add),
                 R=[zTR[zp], constR, cvoR[cs]], W=[cvoR[cs]])
            S.op("dve", lambda e: e.scalar_tensor_tensor(out=cvo[cs][:, 511:512], in0=zT[zn][:, c, 0:1],
                                                         scalar=w2m[:, c, b:b + 1], in1=cvo[cs][:, 511:512],
                                                         op0=ALU.mult, op1=ALU.add),
                 R=[zTR[zn], constR, cvoR[cs]], W=[cvoR[cs]])
            S.op("dve", lambda e: e.scalar_tensor_tensor(out=cT[:, c, 0:511], in0=zT[ks][:, c, 1:512],
                                                         scalar=cw[:, c * 3 + 2:c * 3 + 3], in1=cvo[cs][:, 0:511],
                                                         op0=ALU.mult, op1=ALU.add),
                 R=[zTR[ks], constR, cvoR[cs]], W=[cTR])
            S.op("dve", lambda e: e.tensor_copy(out=cT[:, c, 511:512], in_=cvo[cs][:, 511:512]),
                 R=[cvoR[cs]], W=[cTR])

        def B_bg(b):
            xs = b % 2
            xr = xnTR[xs]
            wt, wr = wnext(("win", 4))
            w3 = wt[:, :].rearrange("p (k w) -> p k w", k=8)
            for c in range(4):
                bk = c % 2
                mm_group(bank(bk), bankR[bk], [(w3[:, kc, c * 128:(c + 1) * 128], xnT[xs][:, kc, :]) for kc in range(8)],
                         [wr] + xr)
                S.op("dve", lambda e, c=c, bk=bk: e.tensor_tensor(out=cT[:, c, :], in0=bank(bk), in1=cT[:, c, :], op=ALU.mult),
                     R=[bankR[bk], cTR], W=[cTR])

        def B_attn(b):
            chunks = []
            for i in range(4):
                olist = [-2, -1, 0, 1, 2]
                if i == 0:
                    olist = olist + [3]
                if i == 3:
                    olist = [-3] + olist
                olist = sorted(olist)
                for s_, o in enumerate(olist):
                    chunks.append((i, s_, o, len(olist)))

            def sbanks_of(n):
                return (2, 3) if n % 2 == 0 else (4, 5)

            def emit_S(n):
                i, s_, o, L = chunks[n]
                g2 = 4 * b + i + o
                b2, i2 = g2 // 4, g2 % 4
                k2 = b2 % 3
                sbk = sbanks_of(n)
                for h in range(NH):
                    for qr in range(2):
                        oap = bank(sbk[qr])[:, h * 64:(h + 1) * 64]
                        S.op("pe", lambda e, oap=oap, h=h, k2=k2, i2=i2, i=i, qr=qr: e.matmul(
                            oap, kT[k2][:, h // 2, i2 * 128:(i2 + 1) * 128],
                            qT[:, h // 2, h % 2, i * 128 + qr * 64:i * 128 + qr * 64 + 64], start=True, stop=True),
                            R=[kTR[k2], qTR], W=[bankR[sbk[qr]]])

            def emit_rest(n):
                i, s_, o, L = chunks[n]
                g2 = 4 * b + i + o
                b2, i2 = g2 // 4, g2 % 4
                k2 = b2 % 3
                sb_ = n % 2
                sbk = sbanks_of(n)
                tl = (b - 1) * 4 + i
                pvb = (6, 7) if i % 2 == 0 else (0, 1)
                pe4 = pexp[sb_][:, :].rearrange("p (h r q) -> p h r q", h=NH, r=2)
                for qr in range(2):
                    col = tl * 12 + s_ * 2 + qr
                    S.op("act", lambda e, pe4=pe4, qr=qr, col=col, bk=sbk[qr]: e.activation(
                        out=pe4[:, :, qr, :], in_=bank(bk).rearrange("p (h q) -> p h q", h=NH), func=AF.Exp,
                        bias=rbias[:, col:col + 1]),
                        R=[bankR[sbk[qr]], constR], W=[pexpR[sb_]])
                S.op("dve", lambda e, sb_=sb_, o=o: e.tensor_tensor(out=pT[sb_][:, :], in0=pexp[sb_][:, :], in1=er[:, o + 3, :],
                                                                   op=ALU.mult),
                     R=[pexpR[sb_], erR], W=[pTR[sb_]])
                for h in range(NH):
                    bk = pvb[h // 4]
                    oap = bank(bk)[:, (h % 4) * 65:(h % 4) * 65 + 65]
                    S.op("pe", lambda e, oap=oap, sb_=sb_, h=h, k2=k2, i2=i2, st_=(s_ == 0 and h % 4 == 0),
                         sp_=(s_ == L - 1): e.matmul(
                        oap, pT[sb_][:, h * 128:(h + 1) * 128], vv[k2][:, i2, h, :], start=st_, stop=sp_,
                        skip_group_check=True),
                        R=[pTR[sb_], vR[k2][i2]], W=[bankR[bk]])
                if s_ == L - 1:
                    ri = i % 2
                    pv4 = ps[:, pvb[0] * 512:pvb[0] * 512 + 1024].rearrange("p (b w) -> p b w", b=2)[:, :, 0:260].rearrange(
                        "p b (h d) -> p b h d", h=4)
                    S.op("dve", lambda e, ri=ri, pv4=pv4: e.reciprocal(out=rden[ri][:, :].rearrange("p (b h) -> p b h", b=2),
                                                                      in_=pv4[:, :, :, 64]),
                         R=[bankR[pvb[0]], bankR[pvb[1]]], W=[rdenR[ri]])
                    for hb in range(2):
                        S.op("dve", lambda e, ri=ri, hb=hb, pv4=pv4: e.tensor_tensor(
                            out=atm[ri][:, hb * 256:(hb + 1) * 256].rearrange("p (h d) -> p h d", h=4),
                            in0=pv4[:, hb, :, 0:64],
                            in1=rden[ri][:, hb * 4:(hb + 1) * 4].unsqueeze(2).to_broadcast([128, 4, 64]), op=ALU.mult),
                            R=[bankR[pvb[hb]], rdenR[ri]], W=[atmR[ri]])
                    deferred.append(lambda i=i, ri=ri, pvb=pvb: transpose_to(
                        atm[ri], atmR[ri], 4, lambda: aT[:, :, i * 128:(i + 1) * 128], aTR, "dve", tb=pvb[0]))

            deferred = []
            emit_S(0)
            for n in range(len(chunks)):
                if n + 1 < len(chunks):
                    emit_S(n + 1)
                if n == 0:
                    B_bg(b)
                pending = list(deferred)
                del deferred[:]
                emit_rest(n)
                for f_ in pending:
                    f_()
            for f_ in deferred:
                f_()


        def B_merge(b):
            xs = b % 2
            xr = xnTR[xs]
            cnt["mm"] = 2
            for half in range(2):
                wbr_t, wbr_r = wnext(("wbr", half))
                wbr3 = wbr_t[:, :].rearrange("p (k w) -> p k w", k=8)
                for k2 in range(2):
                    kq = 2 * half + k2
                    wg_t, wg_r = wnext(("win", 6 + kq), keep=1 + k2)
                    wg = wg_t[:, :].rearrange("p (k j u w) -> p k j u w", k=8, j=2, u=2)
                    for j in range(2):
                        c = 2 * kq + j
                        cc = c % 4
                        gs = c % 2
                        bk = next_mm()
                        mm_group(bank(bk), bankR[bk], [(wg[:, kc, j, 0, :], xnT[xs][:, kc, :]) for kc in range(8)], [wg_r] + xr)
                        S.op("act", lambda e, gs=gs, bk=bk, c=c: e.activation(out=ga[gs][:, :], in_=bank(bk), func=AF.Sigmoid,
                                                                            bias=bg16[:, c:c + 1]),
                             R=[bankR[bk], constR], W=[gaR[gs]])
                        bk = next_mm()
                        mm_group(bank(bk), bankR[bk], [(wg[:, kc, j, 1, :], xnT[xs][:, kc, :]) for kc in range(8)], [wg_r] + xr)
                        S.op("act", lambda e, gs=gs, bk=bk, c=c: e.activation(out=gc[gs][:, :], in_=bank(bk), func=AF.Sigmoid,
                                                                            bias=bg16[:, 8 + c:9 + c]),
                             R=[bankR[bk], constR], W=[gcR[gs]])
                        bk = next_mm()
                        mm_group(bank(bk), bankR[bk], [(wbr3[:, kc, cc * 128:(cc + 1) * 128], aT[:, kc, :]) for kc in range(4)],
                                 [wbr_r, aTR])
                        S.op("dve", lambda e, gs=gs, bk=bk: e.tensor_tensor(out=ga[gs][:, :], in0=bank(bk), in1=ga[gs][:, :],
                                                                           op=ALU.mult),
                             R=[bankR[bk], gaR[gs]], W=[gaR[gs]])
                        bk = next_mm()
                        mm_group(bank(bk), bankR[bk], [(wbr3[:, 4 + kc, cc * 128:(cc + 1) * 128], cT[:, kc, :]) for kc in range(4)],
                                 [wbr_r, cTR])
                        S.op("dve", lambda e, gs=gs, bk=bk: e.tensor_tensor(out=gc[gs][:, :], in0=bank(bk), in1=gc[gs][:, :],
                                                                           op=ALU.mult),
                             R=[bankR[bk], gcR[gs]], W=[gcR[gs]])
                        S.op("pool", lambda e, gs=gs, c=c: e.tensor_tensor(out=mx[:, c, :], in0=ga[gs][:, :], in1=gc[gs][:, :],
                                                                          op=ALU.add),
                             R=[gaR[gs], gcR[gs]], W=mxR)

        def B_wout(b):
            w0t, w0r = wnext(("wout", 0))
            w1t, w1r = wnext(("wout", 1), keep=1)
            wo = [w0t[:, :].rearrange("p (k w) -> p k w", k=8), w1t[:, :].rearrange("p (k w) -> p k w", k=8)]
            wor = [w0r, w1r]
            pend = None
            for i in range(4):
                for n in range(2):
                    bk = next_mm()
                    mm_group(bank(bk), bankR[bk], [(mx[:, kc, i * 128:(i + 1) * 128], wo[n][:, kc, :]) for kc in range(8)],
                             [wor[n], mxR[i]])
                    S.op("dve", lambda e, i=i, n=n, bk=bk: e.tensor_tensor(out=xw[:, i, n * 512:(n + 1) * 512], in0=bank(bk),
                                                                         in1=xw[:, i, n * 512:(n + 1) * 512], op=ALU.add),
                         R=[bankR[bk], xwR[i]], W=[xwR[i]])
                if pend is not None:
                    pend()
                xi = i % 2
                k = rstd_of(xw[:, i, :], xwR[i], xntm[xi][:, :], xntmR[xi])
                S.op("dve", lambda e, xi=xi, k=k, i=i: e.tensor_scalar(out=xntm[xi][:, :], in0=xw[:, i, :],
                                                                      scalar1=st[:, k, 2:3], scalar2=None, op0=ALU.mult),
                     R=[xwR[i], stR[k]], W=[xntmR[xi]])
                pend = (lambda i=i, xi=xi: transpose_to(xntm[xi], xntmR[xi], 8, lambda: mx[:, :, i * 128:(i + 1) * 128],
                                                        mxR[i], "dve"))
            return pend

        def B_ffn(b, hooks):
            stores = []
            for ph in range(3):
                ngi = 4 if ph < 2 else 3
                for gi in range(ngi):
                    wt, wr = wnext(("wfi", ph * 4 + gi))
                    w5 = wt[:, :].rearrange("p (k j u w) -> p k j u w", k=8, j=2, u=2)
                    G = ph * 4 + gi
                    if G in hooks:
                        for f_ in hooks[G][0]:
                            f_()
                    for jj in range(2):
                        jl = gi * 2 + jj
                        ss = jl % 2
                        bk = next_mm()
                        mm_group(bank(bk), bankR[bk], [(w5[:, kc, jj, 0, :], mx[:, kc, :]) for kc in range(8)], [wr] + mxR)
                        S.op("act", lambda e, ss=ss, bk=bk: e.activation(out=sg[ss][:, :], in_=bank(bk), func=AF.Silu),
                             R=[bankR[bk]], W=[sgR[ss]])
                        bk = next_mm()
                        mm_group(bank(bk), bankR[bk], [(w5[:, kc, jj, 1, :], mx[:, kc, :]) for kc in range(8)], [wr] + mxR)
                        S.op("dve", lambda e, ss=ss, bk=bk, jl=jl: e.tensor_tensor(out=hT[:, jl, :], in0=bank(bk), in1=sg[ss][:, :],
                                                                                 op=ALU.mult),
                             R=[bankR[bk], sgR[ss]], W=[hTR])
                    if G in hooks:
                        for f_ in hooks[G][1]:
                            f_()
                nk = 2 * ngi
                w0t, w0r = wnext(("wfd", ph, 0))
                w1t, w1r = wnext(("wfd", ph, 1), keep=1)
                wd = [w0t[:, :].rearrange("p (k w) -> p k w", k=8), w1t[:, :].rearrange("p (k w) -> p k w", k=8)]
                wdr = [w0r, w1r]
                for i in range(4):
                    for n in range(2):
                        bk = next_mm()
                        mm_group(bank(bk), bankR[bk], [(hT[:, kc, i * 128:(i + 1) * 128], wd[n][:, kc, :]) for kc in range(nk)],
                                 [wdr[n], hTR])
                        S.op("dve", lambda e, i=i, n=n, bk=bk: e.tensor_tensor(out=xw[:, i, n * 512:(n + 1) * 512], in0=bank(bk),
                                                                             in1=xw[:, i, n * 512:(n + 1) * 512], op=ALU.add),
                             R=[bankR[bk], xwR[i]], W=[xwR[i]])
                    if ph == 2:
                        xi = i % 2
                        k = rstd_of(xw[:, i, :], xwR[i], xntm[xi][:, :], xntmR[xi])
                        S.op("dve", lambda e, k=k, i=i: e.scalar_tensor_tensor(out=xw[:, i, :], in0=xw[:, i, :], scalar=st[:, k, 2:3],
                                                                              in1=gfin[:, :], op0=ALU.mult, op1=ALU.mult),
                             R=[xwR[i], stR[k], constR], W=[xwR[i]])
                        tok0 = ((b - 1) * 4 + i) * 128
                        stores.append(S.op(XQ, lambda e, i=i, tok0=tok0: e.dma_start(out=y_d[tok0:tok0 + 128, :], in_=xw[:, i, :]),
                                           R=[xwR[i]], dma=f"xw{i}"))
            return stores

        final_stores = []
        for b0 in range(2):
            for i in range(4):
                preA_ew(b0, i)
                preA_tr(b0, i)
        for t in range(NBT):
            matA(t)
            hasB = 1 <= t - 1 <= NB
            nxt = t + 1 < NBT
            if t == 0:
                for i in range(4):
                    preA_ew(2, i)
                    preA_tr(2, i)
                continue
            if t == 1:
                continue
            if hasB:
                for c in range(4):
                    conv_ew(t - 1, c)
                B_front(t - 1)
                B_attn(t - 1)
                B_merge(t - 1)
                pend = B_wout(t - 1)
                pend()
                hooks = {}
                if nxt:
                    ew = [(lambda i=i, t=t: preA_ew(t + 1, i)) for i in range(4)]
                    tr = [(lambda i=i, t=t: preA_tr(t + 1, i)) for i in range(4)]
                    hooks = {0: ([ew[0]], []), 1: ([ew[1]], [tr[0]]), 2: ([ew[2]], [tr[1]]), 3: ([ew[3]], [tr[2]]),
                             4: ([], [tr[3]])}
                final_stores = B_ffn(t - 1, hooks)
            elif nxt:
                for i in range(4):
                    preA_ew(t + 1, i)
                    preA_tr(t + 1, i)
        assert wstate["use"] == len(wseq)
        S.barrier_op("sp", final_stores)
        S.finalize()

        keys = set()
        for e_ in S.ENGS:
            for o in S.q[e_]:
                keys.add(o.key)
        sems = {k: es.enter_context(nc.semaphore(f"s_{k}")) for k in sorted(keys)}
        block = es.enter_context(nc.Block())

        @block.sync
        def _(eng):
            S.emit("sp", eng, sems)

        @block.tensor
        def _(eng):
            S.emit("pe", eng, sems)

        @block.scalar
        def _(eng):
            S.emit("act", eng, sems)

        @block.vector
        def _(eng):
            S.emit("dve", eng, sems)

        @block.gpsimd
        def _(eng):
            S.emit("pool", eng, sems)

    return nc


def host_consts(core, NB, RS, total_rows):
    RC = NB * 8
    rb = np.zeros((128, NB * 48), np.float32)
    kr_of_p = (np.arange(128) // 64)
    for b in range(1, NB + 1):
        for i in range(4):
            tl = (b - 1) * 4 + i
            r0 = core * RC + 8 * (b - 1) + 2 * i
            olist = [-2, -1, 0, 1, 2]
            if i == 0:
                olist = olist + [3]
            if i == 3:
                olist = [-3] + olist
            for s_, o in enumerate(sorted(olist)):
                for qr in range(2):
                    qrow = r0 + qr
                    seq, r = qrow // RS, qrow % RS
                    rs = min(max(r - 4, 0), RS - 8)
                    krow = r0 + 2 * o + kr_of_p
                    ok = (krow >= 0) & (krow < total_rows) & (krow // RS == seq) & (krow % RS >= rs) & (krow % RS < rs + 8)
                    rb[:, tl * 12 + s_ * 2 + qr] = np.where(ok, 0.0, NEG)
    cm = np.ones((128, 2 * (NB + 2)), np.float32)
    for b in range(1, NB + 1):
        rg = core * RC + 8 * (b - 1)
        if rg % RS == 0:
            cm[:, 2 * b] = 0.0
        if (rg + 8) % RS == 0:
            cm[:, 2 * b + 1] = 0.0
    return rb, cm


def run_layer(xall, RS, w, NB, dbg=0):
    import ml_dtypes
    total_rows = xall.shape[0] // GW
    RC = NB * 8
    assert RC * NCORES == total_rows
    nc = build_program(NB, dbg)
    xpad = np.zeros(((total_rows + 16) * GW, D), np.float32)
    xpad[8 * GW:8 * GW + xall.shape[0]] = xall
    g_mix = np.asarray(w["norm_mix_g"], np.float32).reshape(D)
    g_ffn = np.asarray(w["norm_ffn_g"], np.float32).reshape(D)
    g_fin = np.asarray(w["norm_final_g"], np.float32).reshape(D)
    b_gate = np.asarray(w["b_gate"], np.float32).reshape(2 * D)
    conv_w = np.asarray(w["conv_w"], np.float32).reshape(3, DC)
    rpb = np.asarray(w["rpb"], np.float32).reshape(NH, 15, 31)
    kc_ = np.arange(64)[:, None]
    qc_ = np.arange(64)[None, :]
    idx = np.clip(kc_ - qc_ + 15, 0, 30)
    rpbx = np.ascontiguousarray(np.transpose(rpb[:, :, idx], (2, 0, 1, 3)))
    rpbx = np.concatenate([rpbx, rpbx], axis=0)
    cstart = np.clip(qc_ - 8, 0, GW - 16)
    cv = ((kc_ >= cstart) & (kc_ < cstart + 16)).astype(np.float32)
    cv = np.concatenate([cv, cv], axis=0)
    common = {
        "w_in": np.ascontiguousarray(np.asarray(w["w_in"], np.float32).reshape(D, 5120)),
        "w_ab": np.ascontiguousarray(np.asarray(w["w_attn_branch"], np.float32).reshape(DA, D)),
        "w_cb": np.ascontiguousarray(np.asarray(w["w_conv_branch"], np.float32).reshape(DC, D)),
        "w_out": np.ascontiguousarray(np.asarray(w["w_out"], np.float32).reshape(D, D)),
        "w_fi": np.ascontiguousarray(np.asarray(w["w_ffn_in"], np.float32).reshape(D, 2 * DFF)),
        "w_fd": np.ascontiguousarray(np.asarray(w["w_ffn_down"], np.float32).reshape(DFF, D)),
        "g8": np.ascontiguousarray(g_mix.reshape(8, 128).T),
        "gf8": np.ascontiguousarray(g_ffn.reshape(8, 128).T),
        "gfin": np.ascontiguousarray(np.broadcast_to(g_fin[None, :], (128, D))),
        "bg16": np.ascontiguousarray(b_gate.reshape(16, 128).T),
        "cw": np.ascontiguousarray(conv_w.reshape(3, 4, 128).transpose(2, 1, 0).reshape(128, 12)),
        "rpbx": rpbx,
        "cv": cv,
        "ident": np.eye(128, dtype=np.float32).astype(ml_dtypes.bfloat16),
    }
    in_maps = []
    for c in range(NCORES):
        rb, cm = host_consts(c, NB, RS, total_rows)
        m = dict(common)
        m["x"] = np.ascontiguousarray(xpad[c * RC * GW:(c * RC + RC + 16) * GW])
        m["rb"] = rb
        m["cm"] = cm
        in_maps.append(m)
    res = run_bass_kernel_spmd(nc, in_maps, core_ids=list(range(NCORES)))
    return np.concatenate([np.asarray(r["y"], np.float32) for r in res.results], axis=0)


def kernel(x_prompt, x_sample, norm_mix_g, w_in, b_gate, rpb, conv_w, w_attn_branch, w_conv_branch, w_out,
           norm_ffn_g, w_ffn_in, w_ffn_down, norm_final_g):
    xp = np.asarray(x_prompt, np.float32)
    xs = np.asarray(x_sample, np.float32)
    seq = xp.shape[1]
    RS = seq // GW
    xall = np.concatenate([xp.reshape(-1, D), xs.reshape(-1, D)], axis=0)
    total_rows = xall.shape[0] // GW
    NB = total_rows // NCORES // 8
    w = dict(norm_mix_g=norm_mix_g, w_in=w_in, b_gate=b_gate, rpb=rpb, conv_w=conv_w, w_attn_branch=w_attn_branch,
             w_conv_branch=w_conv_branch, w_out=w_out, norm_ffn_g=norm_ffn_g, w_ffn_in=w_ffn_in, w_ffn_down=w_ffn_down,
             norm_final_g=norm_final_g)
    y = run_layer(xall, RS, w, NB)
    n_p = xp.shape[0] * seq
    return (y[:n_p].reshape(xp.shape), y[n_p:].reshape(xs.shape))
```
